# Optimizing a Trainium2 kernel written in Bass

```python
import jax, jax.numpy as jnp
from jax import lax
import numpy as np

D_MODEL = 4096
BATCH = 4
SEQ = 2048
DEPTH = 1
DEC_BATCH = 128
DEC_SEQ = 4
PAST_LEN = 16384
PAGE_SIZE = 128

D_MIX = D_MODEL
D_POOL = D_MIX // 2
D_GATE = D_MIX - D_POOL
POOL_WINDOWS = (2, 4, 8, 16)
N_POOL_GROUPS = len(POOL_WINDOWS)
POOL_GC = D_POOL // N_POOL_GROUPS
POOL_STATE = max(POOL_WINDOWS) - 1
N_GATE_HEADS = 4
GATE_HD = D_GATE // N_GATE_HEADS
CHUNK = 128
N_MEM = 256
N_XHEADS = 4
X_HD = D_MODEL // N_XHEADS
D_FF = ((-(-8 * D_MODEL // 3) + 255) // 256) * 256
EPS = 1e-6

kernel_name = "hybrid_pool_gmlp_memxattn_decoder_step"


def _rmsnorm(x, g):
    xf = x.astype(jnp.float32)
    xf = xf * lax.rsqrt(jnp.mean(xf * xf, axis=-1, keepdims=True) + EPS)
    return (xf * g.astype(jnp.float32)).astype(x.dtype)


def _pool_mixer(p, prefix, pos0, w_pool, pool_scale):
    B, L, C = p.shape
    P = prefix.shape[1]
    ext = jnp.concatenate([prefix, p], axis=1)
    extf = ext.astype(jnp.float32)
    cs = jnp.pad(jnp.cumsum(extf, axis=1), ((0, 0), (1, 0), (0, 0)))
    end = cs[:, P + 1:P + 1 + L]
    pos = pos0 + jnp.arange(L)
    means = []
    for g, w in enumerate(POOL_WINDOWS):
        sl = slice(g * POOL_GC, (g + 1) * POOL_GC)
        start = cs[:, P + 1 - w:P + 1 - w + L, sl]
        cnt = jnp.minimum(pos + 1, w).astype(jnp.float32)[None, :, None]
        means.append((end[..., sl] - start) / cnt)
    pooled = (jnp.concatenate(means, axis=-1) - extf[:, P:]).astype(p.dtype)
    mixed = jnp.einsum('blgc,gcd->blgd', pooled.reshape(B, L, N_POOL_GROUPS, POOL_GC), w_pool)
    out = mixed.reshape(B, L, C) * pool_scale
    new_state = ext[:, -POOL_STATE:]
    return out, new_state


def _spatial_gate(u, v, w_s, b_s):
    B, L, H, hd = v.shape
    c = min(L, CHUNK)
    n = -(-L // c)
    pad = n * c - L
    vc = jnp.pad(v, ((0, 0), (0, pad), (0, 0), (0, 0))).reshape(B, n, c, H, hd)
    mask = jnp.tril(jnp.ones((c, c), dtype=bool))
    w = jnp.where(mask[None], w_s[:, :c, :c], 0).astype(v.dtype)
    mixed = jnp.einsum('hts,bnshd->bnthd', w, vc) + b_s[:, :c].T[None, None, :, :, None]
    mixed = mixed.reshape(B, n * c, H, hd)[:, :L]
    return u * mixed


def _mem_kv(mem, g_mem, w_k, w_v):
    B, M, _ = mem.shape
    hm = _rmsnorm(mem, g_mem)
    k = (hm @ w_k).reshape(B, M, N_XHEADS, X_HD)
    v = (hm @ w_v).reshape(B, M, N_XHEADS, X_HD)
    return k, v


def _layer(x, mem_k, mem_v, pool_prefix, pos0, g_mix, w_in, g_v, w_pool, pool_scale, w_s, b_s, w_out,
           g_xattn, w_q, w_o, g_ffn, w_gate_up, w_down):
    B, L, _ = x.shape
    h = _rmsnorm(x, g_mix)
    proj = h @ w_in
    p = proj[..., :D_POOL]
    z = jax.nn.gelu(proj[..., D_POOL:], approximate=False)
    u = z[..., :D_GATE].reshape(B, L, N_GATE_HEADS, GATE_HD)
    v = _rmsnorm(z[..., D_GATE:].reshape(B, L, N_GATE_HEADS, GATE_HD), g_v.reshape(N_GATE_HEADS, GATE_HD))
    pool_out, new_pool = _pool_mixer(p, pool_prefix, pos0, w_pool, pool_scale)
    gate_out = _spatial_gate(u, v, w_s, b_s).reshape(B, L, D_GATE)
    x = x + jnp.concatenate([pool_out, gate_out], axis=-1) @ w_out
    hq = _rmsnorm(x, g_xattn)
    q = (hq @ w_q).reshape(B, L, N_XHEADS, X_HD)
    s = jnp.einsum('blhd,bmhd->bhlm', q, mem_k).astype(jnp.float32) * (X_HD ** -0.5)
    a = jax.nn.softmax(s, axis=-1).astype(mem_v.dtype)
    o = jnp.einsum('bhlm,bmhd->blhd', a, mem_v).reshape(B, L, D_MODEL)
    x = x + o @ w_o
    hf = _rmsnorm(x, g_ffn)
    gu = hf @ w_gate_up
    x = x + (jax.nn.silu(gu[..., :D_FF]) * gu[..., D_FF:]) @ w_down
    return x, new_pool, v.reshape(B, L, D_GATE)


def setup_inputs(seed: int = 0) -> dict:
    key = jax.random.key(seed)
    ks = jax.random.split(key, 32)
    f32 = jnp.float32

    def nrm(k, shape, scale):
        return jax.random.normal(k, shape, f32) * scale

    def gain(k, shape):
        return 1.0 + 0.02 * jax.random.normal(k, shape, f32)

    D = D_MODEL
    return {
        "x_prompt": nrm(ks[0], (BATCH, SEQ, D), 1.0),
        "x_sample": nrm(ks[1], (DEC_BATCH, DEC_SEQ, D), 1.0),
        "cache_mem_k": nrm(ks[2], (DEPTH, DEC_BATCH, N_MEM, N_XHEADS, X_HD), 1.0),
        "cache_mem_v": nrm(ks[3], (DEPTH, DEC_BATCH, N_MEM, N_XHEADS, X_HD), 1.0),
        "state_pool": nrm(ks[4], (DEPTH, DEC_BATCH, POOL_STATE, D_POOL), 1.0),
        "mem_prompt": nrm(ks[5], (BATCH, N_MEM, D), 1.0),
        "g_mix": gain(ks[6], (DEPTH, D)),
        "w_in": nrm(ks[7], (DEPTH, D, D_POOL + 2 * D_GATE), D ** -0.5),
        "g_v": gain(ks[8], (DEPTH, D_GATE)),
        "w_pool": nrm(ks[9], (DEPTH, N_POOL_GROUPS, POOL_GC, POOL_GC), POOL_GC ** -0.5),
        "pool_scale": gain(ks[10], (DEPTH, D_POOL)),
        "w_s": nrm(ks[11], (DEPTH, N_GATE_HEADS, CHUNK, CHUNK), CHUNK ** -0.5),
        "b_s": 1.0 + 0.1 * jax.random.normal(ks[12], (DEPTH, N_GATE_HEADS, CHUNK), f32),
        "w_out": nrm(ks[13], (DEPTH, D_MIX, D), D_MIX ** -0.5),
        "g_xattn": gain(ks[14], (DEPTH, D)),
        "g_mem": gain(ks[15], (DEPTH, D)),
        "w_q": nrm(ks[16], (DEPTH, D, D), D ** -0.5),
        "w_k": nrm(ks[17], (DEPTH, D, D), D ** -0.5),
        "w_v": nrm(ks[18], (DEPTH, D, D), D ** -0.5),
        "w_o": nrm(ks[19], (DEPTH, D, D), D ** -0.5),
        "g_ffn": gain(ks[20], (DEPTH, D)),
        "w_gate_up": nrm(ks[21], (DEPTH, D, 2 * D_FF), D ** -0.5),
        "w_down": nrm(ks[22], (DEPTH, D_FF, D), D_FF ** -0.5),
        "g_final": gain(ks[23], (D,)),
    }


def reference(x_prompt, x_sample, cache_mem_k, cache_mem_v, state_pool, mem_prompt, g_mix, w_in, g_v,
              w_pool, pool_scale, w_s, b_s, w_out, g_xattn, g_mem, w_q, w_k, w_v, w_o, g_ffn, w_gate_up,
              w_down, g_final):
    xp, xs = x_prompt, x_sample
    mk_list, mv_list, pool_p_list, pool_s_list, vs_list = [], [], [], [], []
    zero_prefix = jnp.zeros((xp.shape[0], POOL_STATE, D_POOL), xp.dtype)
    for l in range(DEPTH):
        lw = (g_mix[l], w_in[l], g_v[l], w_pool[l], pool_scale[l], w_s[l], b_s[l], w_out[l],
              g_xattn[l], w_q[l], w_o[l], g_ffn[l], w_gate_up[l], w_down[l])
        mk_p, mv_p = _mem_kv(mem_prompt, g_mem[l], w_k[l], w_v[l])
        xp, pool_p, _ = _layer(xp, mk_p, mv_p, zero_prefix, 0, *lw)
        xs, pool_s, v_s = _layer(xs, cache_mem_k[l], cache_mem_v[l], state_pool[l], PAST_LEN, *lw)
        mk_list.append(mk_p)
        mv_list.append(mv_p)
        pool_p_list.append(pool_p)
        pool_s_list.append(pool_s)
        vs_list.append(v_s)
    y_prompt = _rmsnorm(xp, g_final)
    y_sample = _rmsnorm(xs, g_final)
    return (y_prompt, y_sample, jnp.stack(mk_list), jnp.stack(mv_list), jnp.stack(pool_p_list),
            jnp.stack(pool_s_list), jnp.stack(vs_list))
```

```python
import contextlib
import numpy as np
import ml_dtypes
import concourse.bass as bass
import concourse.mybir as mybir
from concourse.bass_utils import run_bass_kernel_spmd

F32, BF16 = mybir.dt.float32, mybir.dt.bfloat16
AF = mybir.ActivationFunctionType
ALU = mybir.AluOpType
AX = mybir.AxisListType

D = 4096
DP = 2048
DG = 2048
DFF = 11008
NMEM = 256
EPS = 1e-6
WINS = (2, 4, 8, 16)
SCALE = 1024 ** -0.5
NCORES = 8
GROUPS = [[('p', 0), ('p', 1), ('p', 2)], [('p', 3), ('p', 4), ('p', 5)], [('p', 6), ('p', 7), ('s', 0)]]
NSTG, NSLB = 4, 4
SE = 2048


import os
KSTOP = os.environ.get('KSTOP', '')
STOREQ = os.environ.get('STOREQ', 'act')


class StopBuild(Exception):
    pass


def stage(name):
    if KSTOP and KSTOP == name:
        raise StopBuild()


class Buf:
    def __init__(self, name, excl=False):
        self.name = name
        self.w = None
        self.r = []
        self.excl = excl


class Eng:
    def __init__(self, name, key):
        self.name, self.key = name, key
        self.count = 0
        self.seen = {}
        self.ops = []
        self.pend = [key, None]


class Prog:
    def __init__(self, plan_mode, plan=None):
        self.plan_mode = plan_mode
        self.plan = [] if plan_mode else plan
        self.E = {n: Eng(n, n) for n in ('pe', 'act', 'dve', 'sp')}
        self.dcount = {}
        self.dlast = {}
        self.si = 0
        self.n_dma = 0
        self.n_cast = 0
        self.ps_i = 0
        self.store_i = 0
        self.alt = 0

    def _waits(self, E, reads, writes, extra=()):
        evs = []
        for b in reads:
            if b.w is not None:
                evs.append(b.w)
            if b.excl:
                evs.extend(ev for ev in b.r if ev[0] != E.key)
        for b in writes:
            if b.w is not None:
                evs.append(b.w)
            evs.extend(b.r)
        evs.extend(extra)
        waits = []
        for ev in evs:
            key, val = ev
            if val is None:
                assert key == E.key == 'pe', (key, E.name)
                continue
            if key == E.key and E.name == 'pe':
                continue
            if E.seen.get(key, 0) >= val:
                continue
            E.seen[key] = val
            waits.append((key, val))
        return waits

    def _post(self, ev, reads, writes):
        for b in reads:
            b.r.append(ev)
        for b in writes:
            b.w = ev
            b.r = []

    def emit(self, eng, fn, reads=(), writes=(), sig=True):
        if self.plan_mode:
            return None
        E = self.E[eng]
        waits = self._waits(E, reads, writes)
        if sig:
            E.count += 1
            ev = [E.key, E.count]
            E.pend[1] = E.count
            E.pend = [E.key, None]
            E.ops.append((waits, fn, (E.key, 1)))
        else:
            ev = E.pend
            E.ops.append((waits, fn, None))
        self._post(ev, reads, writes)
        return ev

    def dma(self, q, semkey, out, in_, reads=(), writes=()):
        if self.plan_mode:
            return None
        E = self.E[q]
        extra = [self.dlast[semkey]] if semkey in self.dlast else []
        waits = self._waits(E, reads, writes, extra)
        self.dcount[semkey] = self.dcount.get(semkey, 0) + 16
        ev = [semkey, self.dcount[semkey]]
        self.dlast[semkey] = ev
        E.ops.append((waits, (lambda e, o=out, i=in_: e.dma_start(out=o, in_=i)), (semkey, 16)))
        self._post(ev, reads, writes)
        return ev


def alias(new_bufs, old_bufs):
    evs = []
    for b in old_bufs:
        if b.w is not None:
            evs.append(b.w)
        evs.extend(b.r)
    for b in new_bufs:
        b.r.extend(evs)


def build_program():
    nc = bass.Bass("TRN2", target_bir_lowering=False)

    def din(name, shape, dt=F32):
        return nc.dram_tensor(name, list(shape), dt, kind="ExternalInput").ap()

    def dout(name, shape):
        return nc.dram_tensor(name, list(shape), F32, kind="ExternalOutput").ap()

    xp, xh, xs = din("xp", [1024, D]), din("xh", [128, D]), din("xs", [64, D])
    memx = din("memx", [NMEM, D])
    ck, cv = din("ck", [16, NMEM, D]), din("cv", [16, NMEM, D])
    stp = din("stp", [16, 15, DP])
    w_in, w_pool = din("w_in", [D, 6144]), din("w_pool", [2048, 512])
    w_out, w_q, w_k, w_v, w_o = (din(n, [D, D]) for n in ("w_out", "w_q", "w_k", "w_v", "w_o"))
    w_gu, w_down = din("w_gu", [D, 2 * DFF]), din("w_down", [DFF, D])
    smallA_d, smallB_d = din("smallA", [128, 152]), din("smallB", [128, 1536])
    gfin_d, gv_d = din("gfin_bc", [128, D]), din("gv_bc", [128, DG])
    cbf_d = din("cbf", [128, 2432], BF16)
    yp, ys = dout("yp", [1024, D]), dout("ys", [64, D])
    mk_o, mv_o = dout("mk", [NMEM, D]), dout("mv", [NMEM, D])
    psp_o, pss_o, gvs_o = dout("psp", [15, DP]), dout("pss", [16, 15, DP]), dout("gvs", [64, DG])
    kvs = nc.dram_tensor("kvscratch", [128, 16384], BF16, kind="Internal").ap()

    es = contextlib.ExitStack()
    with es:
        def sb(name, shape, dt):
            return es.enter_context(nc.sbuf_tensor(name, list(shape), dt))

        X = [sb(f"X{i}", [128, D], F32) for i in range(3)]
        HT = sb("HT", [128, 32, 384], BF16)
        B2 = sb("B2", [128, 32, 384], BF16)
        SLAB = [sb(f"SLAB{i}", [128, SE], BF16) for i in range(NSLB)]
        STG = [sb(f"STG{i}", [128, SE], F32) for i in range(NSTG)]
        R1 = sb("R1", [128, 18432], BF16)
        PPREV = sb("PPREV", [128, DP], BF16)
        R3 = sb("R3", [128, 2048], F32)
        TMP = sb("TMP", [128, 2048], F32)
        SMA = sb("SMA", [128, 152], F32)
        CBF = sb("CBF", [128, 2432], BF16)
        WST = sb("WST", [128, 512], BF16)
        WSTS = sb("WSTS", [128, 256], BF16)
        ST = sb("ST", [128, 64], F32)
        PS = [es.enter_context(nc.psum_tensor(f"PS{i}", [128, 512], F32)) for i in range(8)]
        sem_names = ['pe', 'act', 'dve', 'sp'] + [f"w{i}" for i in range(8)] + [f"x{i}" for i in range(3)] + \
                    [f"s{i}" for i in range(4)] + ['misc']
        SEM = {n: es.enter_context(nc.semaphore(n)) for n in sem_names}

        HTf = HT[:, :, :].rearrange("p a b -> p (a b)")
        R3bf = R3[:, :].bitcast(BF16)
        TMPbf = TMP[:, 1024:2048].bitcast(BF16)
        ident = CBF[:, 0:128]
        gcol = lambda gi, k0, nk: SMA[:, gi * 32 + k0: gi * 32 + k0 + nk]
        pscol = lambda c: SMA[:, 128 + c:128 + c + 1]
        P_ = [R1[:, i * 2048:(i + 1) * 2048] for i in range(3)]
        U_ = [R1[:, 6144 + i * 2048: 6144 + (i + 1) * 2048] for i in range(3)]
        V_ = [R1[:, 12288 + i * 2048: 12288 + (i + 1) * 2048] for i in range(3)]
        XN = R1[:, 12288:12288 + 4096]
        KTm = R1[:, 0:8192].rearrange("p (a b) -> p a b", b=256)
        Vm = R1[:, 8192:16384].rearrange("p (a b) -> p a b", b=4096)
        KTb = R1[:, 16384:18432].rearrange("p (a b) -> p a b", b=256)
        ATp = R3bf[:, 0:3072].rearrange("p (a b) -> p a b", b=384)
        pooledT = HTf[:, 0:6144].rearrange("p (a b) -> p a b", b=384)
        Gt = HTf[:, 6144:8192]
        Ef = TMP[:, 0:1024]
        Abf = TMPbf[:, 0:1024]
        ATb = TMPbf[:, 1024:1056]

        def ps3(i, b=128):
            return PS[i][:, :].rearrange("p (a b) -> p a b", b=b)

        def run_all(P):
            try:
                run_all_inner(P)
            except StopBuild:
                pass
            if not P.plan_mode:
                E = P.E[STOREQ]
                waits = []
                for k, ev in P.dlast.items():
                    if k.startswith('s') and k != 'sp':
                        waits.append((k, ev[1]))
                E.ops.append((waits, None, None))

        def run_all_inner(P):
            bX = [Buf(f"X{i}") for i in range(3)]
            bHT = Buf("HT")
            bB2 = Buf("B2")
            bSLAB = [[Buf(f"SLAB{i}a"), Buf(f"SLAB{i}b"), Buf(f"SLAB{i}c")] for i in range(NSLB)]
            bSTG = [Buf(f"STG{i}") for i in range(NSTG)]
            bR1 = Buf("R1")
            bPP = Buf("PPREV")
            bR3 = Buf("R3")
            bTMP = Buf("TMP")
            bCONST = Buf("CONST")
            bST = Buf("ST")
            bPS = [Buf(f"PS{i}", excl=True) for i in range(8)]
            bKVS = Buf("KVS")
            bOUT = Buf("OUT")

            held = set()

            def psum(hold=False):
                while True:
                    i = P.ps_i % 8
                    P.ps_i += 1
                    if i not in held:
                        break
                if hold:
                    held.add(i)
                return i

            def altcopy(out, in_, reads, writes):
                P.alt += 1
                if P.alt % 2:
                    return P.emit('act', lambda e: e.activation(out=out, in_=in_, func=AF.Copy), reads, writes)
                return P.emit('dve', lambda e: e.tensor_copy(out=out, in_=in_), reads, writes)

            def store(out, in_, reads):
                if os.environ.get('NOSTORE'):
                    return None
                k = f"s{P.store_i % 4}"
                P.store_i += 1
                return P.dma(STOREQ, k, out, in_, reads=reads, writes=[bOUT])

            def slab_get(desc, keep=0):
                if P.plan_mode:
                    P.plan.append(desc)
                    return 0
                i = P.si
                P.si += 1
                assert P.plan[i]['tag'] == desc['tag'], (P.plan[i]['tag'], desc['tag'])
                n = len(P.plan)

                def do_dma(j):
                    d = P.plan[j]
                    s = j % NSTG
                    for (src, dstf) in d['srcs']:
                        P.dma('sp', f"w{j % 8}", dstf(STG[s]), src, reads=[], writes=[bSTG[s]])

                def do_cast(j):
                    d = P.plan[j]
                    s, b = j % NSTG, j % NSLB
                    np_, ne = d['np'], d['ne']
                    c = d['cast']
                    if c[0] == 'plain':
                        h_ = (ne // 2) if ne >= 1024 else ne
                        P.emit('act', lambda e: e.activation(out=SLAB[b][0:np_, 0:h_], in_=STG[s][0:np_, 0:h_],
                                                             func=AF.Copy), [bSTG[s]], [bSLAB[b][0]])
                        if h_ < ne:
                            P.emit('dve', lambda e: e.tensor_copy(out=SLAB[b][0:np_, h_:ne], in_=STG[s][0:np_, h_:ne]),
                                   [bSTG[s]], [bSLAB[b][1], bSLAB[b][2]])
                    else:
                        _, gi, k0, nk = c
                        assert nk == 4
                        o3 = SLAB[b][:, 0:1024].rearrange("p (k n) -> p k n", n=512)
                        i3 = STG[s][:, 0:1024].rearrange("p (k n) -> p k n", n=512)
                        g3 = gcol(gi, k0, 2).unsqueeze(2).broadcast_to([128, 2, 512])
                        P.emit('dve', lambda e: e.tensor_tensor(out=o3, in0=i3, in1=g3, op=ALU.mult),
                               [bSTG[s], bCONST], [bSLAB[b][0]])
                        for k_ in (2, 3):
                            P.emit('act', lambda e, k_=k_: e.activation(
                                out=SLAB[b][:, k_ * 512:(k_ + 1) * 512], in_=STG[s][:, k_ * 512:(k_ + 1) * 512],
                                func=AF.Copy, scale=gcol(gi, k0 + k_, 1)), [bSTG[s], bCONST], [bSLAB[b][k_ - 1]])

                while P.n_cast < min(i + NSLB - keep, n):
                    j = P.n_cast
                    while P.n_dma <= j:
                        do_dma(P.n_dma)
                        P.n_dma += 1
                    do_cast(j)
                    P.n_cast += 1
                while P.n_dma < min(P.n_cast + NSTG, n):
                    do_dma(P.n_dma)
                    P.n_dma += 1
                return i % NSLB

            def wdesc(tag, W, r0, nk, pieces, cast):
                srcs = []
                for (c0, ncol, d0) in pieces:
                    src = W[r0:r0 + nk * 128, c0:c0 + ncol].rearrange("(k p) n -> p k n", p=128)
                    srcs.append((src, (lambda stg, d0=d0, ncol=ncol, nk=nk:
                                       stg[:, 0:nk * 512].rearrange("p (k n) -> p k n", n=512)[:, :, d0:d0 + ncol])))
                return dict(tag=tag, srcs=srcs, cast=cast, np=128, ne=nk * 512)

            def norm_T(xb, xt, ntok, dst3, dstb, col0):
                ss, sq, rs = ST[0:ntok, 0:1], ST[0:ntok, 1:2], ST[0:ntok, 2:3]
                P.emit('dve', lambda e: e.memset(ST[:, 0:1], 0.0), [], [bST])
                P.emit('act', lambda e: e.activation(out=XN[0:ntok, :], in_=xt[0:ntok, :], func=AF.Square,
                                                     accum_out=ss), [xb], [bR1, bST])
                P.emit('act', lambda e: e.activation(out=sq, in_=ss, func=AF.Sqrt, scale=1.0 / D, bias=EPS_AP[0:ntok]),
                       [bCONST], [bST])
                P.emit('dve', lambda e: e.reciprocal(out=rs, in_=sq), [], [bST])
                P.emit('act', lambda e: e.activation(out=XN[0:ntok, :], in_=xt[0:ntok, :], func=AF.Copy, scale=rs),
                       [xb, bST], [bR1])
                for kg in range(8):
                    pi = psum()
                    for i in range(4):
                        c = kg * 4 + i
                        P.emit('pe', lambda e, c=c, i=i, pi=pi: e.matmul(
                            PS[pi][:, i * 128:i * 128 + ntok], lhsT=XN[0:ntok, c * 128:(c + 1) * 128],
                            rhs=ident[0:ntok, 0:ntok], start=True, stop=True),
                            [bR1, bCONST], [bPS[pi]], sig=(i == 3))
                    altcopy(dst3[:, kg * 4:kg * 4 + 4, col0:col0 + ntok], ps3(pi)[:, :, 0:ntok], [bPS[pi]], [dstb])

            def linear_B(tag, W, inT, inb, nk_tot, col_blocks, tiles, cast_fn, consumer):
                nq = (nk_tot + 3) // 4
                for jb, (c0, ncol) in enumerate(col_blocks):
                    pis = [psum() for _ in tiles]
                    for q in range(nq):
                        nk = min(4, nk_tot - q * 4)
                        si = slab_get(wdesc((tag, jb, q), W, q * 512, nk, [(c0, ncol, 0)], cast_fn(q * 4, nk)))
                        if P.plan_mode:
                            continue
                        for ti, (tc0, ntok) in enumerate(tiles):
                            for k in range(nk):
                                kk = q * 4 + k
                                last = (kk == nk_tot - 1)
                                P.emit('pe', lambda e, pi=pis[ti], kk=kk, k=k, si=si, tc0=tc0, ntok=ntok, last=last, ncol=ncol:
                                       e.matmul(PS[pi][0:ntok, 0:ncol], lhsT=inT[:, kk, tc0:tc0 + ntok],
                                                rhs=SLAB[si][:, k * 512:k * 512 + ncol], start=(kk == 0), stop=last),
                                       [inb] + bSLAB[si], [bPS[pis[ti]]],
                                       sig=(k == nk - 1 and (last or ti == len(tiles) - 1)))
                    if P.plan_mode:
                        continue
                    for ti in range(len(tiles)):
                        consumer(jb, ti, pis[ti])

            def linear_A(tag, W, inT, inb, nk_tot, slab_cols, Tg, cast_fn, consumer):
                nq = nk_tot // 4
                for js, pieces in enumerate(slab_cols):
                    pis = [psum() for _ in range(4)]
                    for q in range(nq):
                        si = slab_get(wdesc((tag, js, q), W, q * 512, 4, pieces, cast_fn(q * 4, 4)))
                        if P.plan_mode:
                            continue
                        for c in range(4):
                            for k in range(4):
                                kk = q * 4 + k
                                last = (kk == nk_tot - 1)
                                P.emit('pe', lambda e, pi=pis[c], kk=kk, k=k, si=si, c=c, last=last:
                                       e.matmul(PS[pi][:, 0:Tg], lhsT=SLAB[si][:, k * 512 + c * 128:k * 512 + (c + 1) * 128],
                                                rhs=inT[:, kk, 0:Tg], start=(kk == 0), stop=last),
                                       [inb] + bSLAB[si], [bPS[pis[c]]],
                                       sig=(k == 3 and (last or c == 3)))
                    if P.plan_mode:
                        continue
                    consumer(js, pis)

            plain = lambda k0, nk: ('plain',)
            gcast = lambda gi: (lambda k0, nk: ('g', gi, k0, nk))

            P.dma('sp', 'misc', SMA[:, :], smallA_d, [], [bCONST])
            P.dma('sp', 'misc', CBF[:, :], cbf_d, [], [bCONST])
            P.dma('sp', 'misc', TMP[:, 0:1536], smallB_d, [], [bTMP])
            P.emit('dve', lambda e: e.memset(ST[:, 8:9], EPS), [], [bCONST])
            P.emit('dve', lambda e: e.tensor_tensor(out=WST[:, :], in0=TMP[:, 0:512], in1=TMP[:, 512:1024], op=ALU.mult),
                   [bTMP], [bCONST])
            P.emit('dve', lambda e: e.tensor_tensor(out=WSTS[0:64, :], in0=TMP[0:64, 1024:1280],
                                                    in1=TMP[0:64, 1280:1536], op=ALU.mult), [bTMP], [bCONST])

            stage('setup')
            P.dma('sp', 'x0', X[0][:, :], xh, [], [bX[0]])
            norm_T(bX[0], X[0], 128, HT, bHT, 0)

            def halo_cons(jb, ti, pi):
                altcopy(PPREV[:, jb * 512:(jb + 1) * 512], PS[pi][:, :], [bPS[pi]], [bPP])
            linear_B('halo', w_in, HT, bHT, 32, [(j * 512, 512) for j in range(4)], [(0, 128)], gcast(0), halo_cons)

            stage('halo')
            for i in range(2):
                P.dma('sp', f"x{i}", X[i][:, :], memx[i * 128:(i + 1) * 128, :], [], [bX[i]])
            for i in range(2):
                norm_T(bX[i], X[i], 128, HT, bHT, i * 128)
            stage('mem_norm')
            OF = [TMP[:, 0:512], TMP[:, 512:1024]]
            KB = TMPbf[:, 0:512]
            bOF = [Buf("OF0"), Buf("OF1")]
            bKB = Buf("KB")
            alias(bOF + [bKB], [bTMP])
            ofi = [0]

            def k_cons(jb, ti, pi):
                if os.environ.get('MEMVAR') == 'trivial':
                    altcopy(Vm[:, ti, jb * 512:(jb + 1) * 512], PS[pi][:, :], [bPS[pi]], [bR1])
                    return
                o = ofi[0] % 2
                ofi[0] += 1
                mvv = os.environ.get('MEMVAR')
                if mvv != 'b':
                    P.emit('act', lambda e: e.activation(out=OF[o], in_=PS[pi][:, :], func=AF.Copy), [bPS[pi]], [bOF[o]])
                P.emit('dve', lambda e: e.tensor_copy(out=KB, in_=PS[pi][:, :]), [bPS[pi]], [bKB])
                store(mk_o[ti * 128:(ti + 1) * 128, jb * 512:(jb + 1) * 512], OF[o], [bOF[o]])
                if mvv == 'a':
                    return
                p2 = psum()
                for c in range(4):
                    P.emit('pe', lambda e, c=c, p2=p2: e.matmul(PS[p2][:, c * 128:(c + 1) * 128],
                                                                lhsT=KB[:, c * 128:(c + 1) * 128], rhs=ident,
                                                                start=True, stop=True),
                           [bKB, bCONST], [bPS[p2]], sig=(c == 3))
                altcopy(KTm[:, jb * 4:jb * 4 + 4, ti * 128:(ti + 1) * 128], ps3(p2), [bPS[p2]], [bR1])

            def v_cons(jb, ti, pi):
                o = ofi[0] % 2
                ofi[0] += 1
                P.emit('act', lambda e: e.activation(out=OF[o], in_=PS[pi][:, :], func=AF.Copy), [bPS[pi]], [bOF[o]])
                P.emit('dve', lambda e: e.tensor_copy(out=Vm[:, ti, jb * 512:(jb + 1) * 512], in_=PS[pi][:, :]),
                       [bPS[pi]], [bR1])
                store(mv_o[ti * 128:(ti + 1) * 128, jb * 512:(jb + 1) * 512], OF[o], [bOF[o]])

            cb8 = [(j * 512, 512) for j in range(8)]
            mt = [(0, 128), (128, 128)]
            linear_B('wk', w_k, HT, bHT, 32, cb8, mt, gcast(3), k_cons)
            stage('mem_k')
            linear_B('wv', w_v, HT, bHT, 32, cb8, mt, gcast(3), v_cons)
            if not os.environ.get('NOKVS'):
                P.dma(STOREQ, 'misc', kvs, R1[:, 0:16384], [bR1], [bKVS])
            alias([bTMP], bOF + [bKB])

            stage('mem')
            def group_pass(gi, grp):
                tiles = []
                for i, (kind, idx) in enumerate(grp):
                    tiles.append((i * 128, 128 if kind == 'p' else 64))
                Tg = tiles[-1][0] + tiles[-1][1]
                has_s = grp[-1][0] == 's'
                ptiles = [t for t, g in zip(tiles, grp) if g[0] == 'p']
                Tp = len(ptiles) * 128

                for i, (kind, idx) in enumerate(grp):
                    nt = tiles[i][1]
                    src = xp[idx * 128:(idx + 1) * 128, :] if kind == 'p' else xs
                    P.dma('sp', f"x{i}", X[i][0:nt, :], src, [], [bX[i]])
                P.dma('sp', 'misc', R3[:, :], gv_d, [], [bR3])
                for i in range(3):
                    norm_T(bX[i], X[i], tiles[i][1], HT, bHT, tiles[i][0])

                stage(f'g{gi}p0')
                ZV = [TMP[:, 0:512], TMP[:, 512:1024]]
                PF = [TMP[:, 1024:1536], TMP[:, 1536:2048]]
                bZV = [Buf("ZV0"), Buf("ZV1")]
                bPF = [Buf("PF0"), Buf("PF1")]
                alias(bZV + bPF, [bTMP])
                zi = [0]
                pfi = [0]

                def in_cons(jb, ti, pi, grp=grp, tiles=tiles, bZV=bZV, bPF=bPF, ZV=ZV, PF=PF, zi=zi, pfi=pfi):
                    kind, idx = grp[ti]
                    nt = tiles[ti][1]
                    if jb < 4:
                        P.emit('act', lambda e: e.activation(out=P_[ti][0:nt, jb * 512:(jb + 1) * 512],
                                                             in_=PS[pi][0:nt, :], func=AF.Copy), [bPS[pi]], [bR1])
                        if kind == 's' or idx == 7:
                            o = pfi[0] % 2
                            pfi[0] += 1
                            P.emit('dve', lambda e: e.tensor_copy(out=PF[o][0:nt, :], in_=PS[pi][0:nt, :]),
                                   [bPS[pi]], [bPF[o]])
                            if kind == 's':
                                for b in range(16):
                                    store(pss_o[b, 11:15, jb * 512:(jb + 1) * 512], PF[o][b * 4:(b + 1) * 4, :], [bPF[o]])
                            else:
                                store(psp_o[:, jb * 512:(jb + 1) * 512], PF[o][113:128, :], [bPF[o]])
                    elif jb < 8:
                        h = jb - 4
                        P.emit('act', lambda e: e.activation(out=U_[ti][0:nt, h * 512:(h + 1) * 512],
                                                             in_=PS[pi][0:nt, :], func=AF.Gelu), [bPS[pi]], [bR1])
                    else:
                        h = jb - 8
                        o = zi[0] % 2
                        zi[0] += 1
                        z = ZV[o]
                        ss, sq, rs = ST[0:nt, 3:4], ST[0:nt, 4:5], ST[0:nt, 5:6]
                        P.emit('act', lambda e: e.activation(out=z[0:nt, :], in_=PS[pi][0:nt, :], func=AF.Gelu),
                               [bPS[pi]], [bZV[o]])
                        P.emit('dve', lambda e: e.memset(ST[:, 3:4], 0.0), [], [bST])
                        P.emit('dve', lambda e: e.tensor_tensor_scan(
                            out=PS[pi][0:nt, :], data0=z[0:nt, :], data1=z[0:nt, :], initial=0.0,
                            op0=ALU.mult, op1=ALU.add) if False else e.scalar_tensor_tensor(
                            out=PS[pi][0:nt, :], in0=z[0:nt, :], scalar=1.0, in1=z[0:nt, :], op0=ALU.mult,
                            op1=ALU.mult, accum_out=ss), [bZV[o]], [bPS[pi], bST])
                        P.emit('act', lambda e: e.activation(out=sq, in_=ss, func=AF.Sqrt, scale=1.0 / 512,
                                                             bias=EPS_AP[0:nt]), [bCONST], [bST])
                        P.emit('dve', lambda e: e.reciprocal(out=rs, in_=sq), [], [bST])
                        if kind == 's':
                            P.emit('dve', lambda e: e.scalar_tensor_tensor(
                                out=z[0:nt, :], in0=z[0:nt, :], scalar=rs, in1=R3[0:nt, h * 512:(h + 1) * 512],
                                op0=ALU.mult, op1=ALU.mult), [bST, bR3], [bZV[o]])
                            P.emit('dve', lambda e: e.tensor_copy(out=V_[ti][0:nt, h * 512:(h + 1) * 512], in_=z[0:nt, :]),
                                   [bZV[o]], [bR1])
                            store(gvs_o[:, h * 512:(h + 1) * 512], z[0:nt, :], [bZV[o]])
                        else:
                            P.emit('dve', lambda e: e.scalar_tensor_tensor(
                                out=V_[ti][0:nt, h * 512:(h + 1) * 512], in0=z[0:nt, :], scalar=rs,
                                in1=R3[0:nt, h * 512:(h + 1) * 512], op0=ALU.mult, op1=ALU.mult),
                                [bZV[o], bST, bR3], [bR1])

                linear_B(('win', gi), w_in, HT, bHT, 32, [(j * 512, 512) for j in range(12)], tiles, gcast(0), in_cons)
                alias([bTMP], bZV + bPF)

                stage(f'g{gi}p1')
                for g in range(4):
                    pp = []
                    for i, (kind, idx) in enumerate(grp):
                        nt = tiles[i][1]
                        tc0 = tiles[i][0]
                        pi = psum()
                        sst = []
                        if kind == 's':
                            for half in range(2):
                                d = dict(tag=('st', g, half),
                                         srcs=[(stp[half * 8:(half + 1) * 8].rearrange("b r c -> (b r) c")[:, g * 512:(g + 1) * 512],
                                                (lambda stg: stg[0:120, 0:512]))], cast=('plain',), np=120, ne=512)
                                sst.append(slab_get(d, keep=half))
                        if P.plan_mode:
                            continue
                        for cc in range(4):
                            osl = PS[pi][:, cc * 128:cc * 128 + nt]
                            pc = slice(g * 512 + cc * 128, g * 512 + (cc + 1) * 128)
                            if kind == 'p':
                                btc = CBF[:, 128 + g * 128:128 + (g + 1) * 128] if idx == 0 else \
                                    CBF[:, 640 + g * 128:640 + (g + 1) * 128]
                                prev = PPREV if i == 0 else P_[i - 1]
                                prevb = bPP if i == 0 else bR1
                                P.emit('pe', lambda e, osl=osl, pc=pc, btc=btc, i=i: e.matmul(
                                    osl, lhsT=P_[i][:, pc], rhs=btc, start=True, stop=False),
                                    [bR1, bCONST], [bPS[pi]], sig=False)
                                P.emit('pe', lambda e, osl=osl, pc=pc, prev=prev, g=g: e.matmul(
                                    osl, lhsT=prev[:, pc], rhs=CBF[:, 1152 + g * 128:1152 + (g + 1) * 128],
                                    start=False, stop=True), [prevb, bCONST], [bPS[pi]], sig=(cc == 3))
                            else:
                                P.emit('pe', lambda e, osl=osl, pc=pc, i=i, g=g: e.matmul(
                                    osl, lhsT=P_[i][0:64, pc], rhs=CBF[0:64, 1664 + g * 64:1664 + (g + 1) * 64],
                                    start=True, stop=False), [bR1, bCONST], [bPS[pi]], sig=False)
                                for half in range(2):
                                    P.emit('pe', lambda e, osl=osl, cc=cc, half=half, g=g, s_=sst[half]: e.matmul(
                                        osl, lhsT=SLAB[s_][0:120, cc * 128:(cc + 1) * 128],
                                        rhs=CBF[0:120, 1920 + half * 256 + g * 64:1920 + half * 256 + (g + 1) * 64],
                                        start=False, stop=(half == 1)), bSLAB[sst[half]] + [bCONST], [bPS[pi]],
                                        sig=(half == 1))
                        altcopy(pooledT[:, g * 4:(g + 1) * 4, tc0:tc0 + nt], ps3(pi)[:, :, 0:nt], [bPS[pi]], [bHT])
                    si = slab_get(wdesc(('wpool', gi, g), w_pool, g * 512, 4, [(0, 512, 0)], ('plain',)))
                    if P.plan_mode:
                        continue
                    for dc in range(4):
                        pi = psum()
                        for cc in range(4):
                            P.emit('pe', lambda e, pi=pi, cc=cc, dc=dc, si=si, g=g: e.matmul(
                                PS[pi][:, 0:Tg], lhsT=SLAB[si][:, cc * 512 + dc * 128:cc * 512 + (dc + 1) * 128],
                                rhs=pooledT[:, g * 4 + cc, 0:Tg], start=(cc == 0), stop=(cc == 3)),
                                bSLAB[si] + [bHT], [bPS[pi]], sig=(cc == 3))
                        P.emit('act', lambda e, pi=pi, c=g * 4 + dc: e.activation(
                            out=B2[:, c, 0:Tg], in_=PS[pi][:, 0:Tg], func=AF.Copy, scale=pscol(c)),
                            [bPS[pi], bCONST], [bB2])
                if has_s and not P.plan_mode:
                    for b in range(16):
                        store(pss_o[b, 0:11, :], stp[b, 4:15, :], [])
                if gi < 2:
                    P.emit('dve', lambda e: e.tensor_copy(out=PPREV[:, :], in_=P_[2]), [bR1], [bPP])
                for i, (kind, idx) in enumerate(grp):
                    nt, tc0 = tiles[i][1], tiles[i][0]
                    if P.plan_mode:
                        continue
                    for h in range(4):
                        pi = psum()
                        if kind == 'p':
                            wst, bc = WST[:, h * 128:(h + 1) * 128], SMA[:, 144 + h:145 + h]
                        else:
                            wst, bc = WSTS[0:64, h * 64:(h + 1) * 64], SMA[0:64, 148 + h:149 + h]
                        P.emit('pe', lambda e, pi=pi, wst=wst, i=i, h=h, nt=nt: e.matmul(
                            PS[pi][0:nt, :], lhsT=wst, rhs=V_[i][0:nt, h * 512:(h + 1) * 512], start=True, stop=True),
                            [bCONST, bR1], [bPS[pi]])
                        P.emit('dve', lambda e, pi=pi, bc=bc, i=i, h=h, nt=nt: e.scalar_tensor_tensor(
                            out=Gt[0:nt, h * 512:(h + 1) * 512], in0=PS[pi][0:nt, :], scalar=bc,
                            in1=U_[i][0:nt, h * 512:(h + 1) * 512], op0=ALU.add, op1=ALU.mult),
                            [bPS[pi], bCONST, bR1], [bHT])
                    for h in range(4):
                        pi = psum()
                        for cc in range(4):
                            P.emit('pe', lambda e, pi=pi, cc=cc, h=h, nt=nt: e.matmul(
                                PS[pi][:, cc * 128:cc * 128 + nt],
                                lhsT=Gt[0:nt, h * 512 + cc * 128:h * 512 + (cc + 1) * 128], rhs=ident[0:nt, 0:nt],
                                start=True, stop=True), [bHT, bCONST], [bPS[pi]], sig=(cc == 3))
                        altcopy(B2[:, 16 + h * 4:16 + (h + 1) * 4, tc0:tc0 + nt], ps3(pi)[:, :, 0:nt], [bPS[pi]], [bB2])

                stage(f'g{gi}p2')
                def res_cons(jb, ti, pi, tiles=tiles):
                    nt = tiles[ti][1]
                    xs_ = X[ti][0:nt, jb * 512:(jb + 1) * 512]
                    P.emit('dve', lambda e: e.tensor_tensor(out=xs_, in0=PS[pi][0:nt, :], in1=xs_, op=ALU.add),
                           [bPS[pi]], [bX[ti]])
                linear_B(('wout', gi), w_out, B2, bB2, 32, cb8, tiles, plain, res_cons)

                stage(f'g{gi}p3')
                for i in range(3):
                    norm_T(bX[i], X[i], tiles[i][1], HT, bHT, tiles[i][0])

                def q_cons(js, pis):
                    for c in range(4):
                        altcopy(B2[:, js * 4 + c, 0:Tg], PS[pis[c]][:, 0:Tg], [bPS[pis[c]]], [bB2])
                linear_A(('wq', gi), w_q, HT, bHT, 32, [[(j * 512, 512, 0)] for j in range(8)], Tg, gcast(1), q_cons)

                stage(f'g{gi}p5')
                P.dma('sp', 'misc', R1[:, 0:16384], kvs, [bKVS], [bR1])
                bE, bA = Buf("E"), Buf("A")
                alias([bE, bA], [bTMP])
                def attn_prompt_tile(i):
                    tc0 = tiles[i][0]
                    pA, pB = psum(), psum()
                    pp2 = [pA, pB]
                    for h in range(4):
                        for kc in range(8):
                            P.emit('pe', lambda e, h=h, kc=kc, tc0=tc0, pq=pp2[h // 2]: e.matmul(
                                PS[pq][:, (h % 2) * 256:(h % 2) * 256 + 256], lhsT=B2[:, h * 8 + kc, tc0:tc0 + 128],
                                rhs=KTm[:, h * 8 + kc, :], start=(kc == 0), stop=(kc == 7)),
                                [bB2, bR1], [bPS[pp2[h // 2]]], sig=(kc == 7))
                    mx, nm, sm, rsm = ST[:, 16:20], ST[:, 20:24], ST[:, 24:28], ST[:, 28:32]
                    for hp in range(2):
                        P.emit('dve', lambda e, hp=hp: e.tensor_reduce(
                            out=ST[:, 16 + hp * 2:18 + hp * 2], in_=ps3(pp2[hp], 256), axis=AX.X, op=ALU.max),
                            [bPS[pp2[hp]]], [bST])
                    P.emit('dve', lambda e: e.tensor_scalar(out=nm, in0=mx, scalar1=-SCALE, scalar2=None, op0=ALU.mult),
                           [], [bST])
                    P.emit('dve', lambda e: e.memset(sm, 0.0), [], [bST])
                    for h in range(4):
                        P.emit('act', lambda e, h=h: e.activation(
                            out=Ef[:, h * 256:(h + 1) * 256], in_=PS[pp2[h // 2]][:, (h % 2) * 256:(h % 2) * 256 + 256],
                            func=AF.Exp, scale=SCALE, bias=ST[:, 20 + h:21 + h], accum_out=ST[:, 24 + h:25 + h]),
                            [bPS[pp2[h // 2]], bST], [bE, bST])
                    P.emit('dve', lambda e: e.reciprocal(out=rsm, in_=sm), [], [bST])
                    for h in range(4):
                        P.emit('dve', lambda e, h=h: e.tensor_scalar(
                            out=Abf[:, h * 256:(h + 1) * 256], in0=Ef[:, h * 256:(h + 1) * 256],
                            scalar1=ST[:, 28 + h:29 + h], scalar2=None, op0=ALU.mult), [bE, bST], [bA])
                    for half in range(2):
                        pi = psum()
                        for j in range(4):
                            hm = half * 4 + j
                            P.emit('pe', lambda e, pi=pi, j=j, hm=hm: e.matmul(
                                PS[pi][:, j * 128:(j + 1) * 128], lhsT=Abf[:, hm * 128:(hm + 1) * 128], rhs=ident,
                                start=True, stop=True), [bA, bCONST], [bPS[pi]], sig=(j == 3))
                        altcopy(ATp[:, half * 4:half * 4 + 4, tc0:tc0 + 128], ps3(pi), [bPS[pi]], [bR3])
                for i, (kind, idx) in enumerate(grp):
                    if kind == 'p' and not P.plan_mode:
                        attn_prompt_tile(i)
                if not P.plan_mode:
                    for c in range(32):
                        h = c // 8
                        pi = psum()
                        for mc in range(2):
                            P.emit('pe', lambda e, pi=pi, mc=mc, c=c, h=h: e.matmul(
                                PS[pi][:, 0:Tp], lhsT=Vm[:, mc, c * 128:(c + 1) * 128], rhs=ATp[:, h * 2 + mc, 0:Tp],
                                start=(mc == 0), stop=(mc == 1)), [bR1, bR3], [bPS[pi]], sig=(mc == 1))
                        altcopy(HT[:, c, 0:Tp], PS[pi][:, 0:Tp], [bPS[pi]], [bHT])
                if has_s:
                    sc0 = tiles[-1][0]
                    def attn_sample_seq(b):
                        q0 = sc0 + 4 * b
                        pS = [psum(True), psum(True)]
                        for h in range(4):
                            d = dict(tag=('ck', b, h), srcs=[(ck[b, :, h * 1024:(h + 1) * 1024].rearrange(
                                "(m p) d -> p m d", p=128), (lambda stg: stg[:, :].rearrange("p (m d) -> p m d", d=1024)))],
                                cast=('plain',), np=128, ne=2048)
                            si = slab_get(d)
                            if P.plan_mode:
                                continue
                            for dp in range(4):
                                pi = psum()
                                for dd in range(2):
                                    dc = dp * 2 + dd
                                    for mc in range(2):
                                        P.emit('pe', lambda e, pi=pi, dd=dd, mc=mc, dc=dc, si=si: e.matmul(
                                            PS[pi][:, dd * 256 + mc * 128:dd * 256 + (mc + 1) * 128],
                                            lhsT=SLAB[si][:, mc * 1024 + dc * 128:mc * 1024 + (dc + 1) * 128], rhs=ident,
                                            start=True, stop=True), bSLAB[si] + [bCONST], [bPS[pi]],
                                            sig=(dd == 1 and mc == 1))
                                altcopy(KTb[:, dp * 2:dp * 2 + 2, :], ps3(pi, 256), [bPS[pi]], [bR1])
                            for kc in range(8):
                                P.emit('pe', lambda e, h=h, kc=kc, q0=q0, pq=pS[h // 2]: e.matmul(
                                    PS[pq][0:4, (h % 2) * 256:(h % 2) * 256 + 256], lhsT=B2[:, h * 8 + kc, q0:q0 + 4],
                                    rhs=KTb[:, kc, :], start=(kc == 0), stop=(kc == 7)),
                                    [bB2, bR1], [bPS[pS[h // 2]]], sig=(kc == 7))
                        if P.plan_mode:
                            held.discard(pS[0])
                            held.discard(pS[1])
                        if not P.plan_mode:
                            for hp in range(2):
                                P.emit('dve', lambda e, hp=hp: e.tensor_reduce(
                                    out=ST[0:4, 16 + hp * 2:18 + hp * 2], in_=ps3(pS[hp], 256)[0:4], axis=AX.X,
                                    op=ALU.max), [bPS[pS[hp]]], [bST])
                            P.emit('dve', lambda e: e.tensor_scalar(out=ST[0:4, 20:24], in0=ST[0:4, 16:20],
                                                                    scalar1=-SCALE, scalar2=None, op0=ALU.mult),
                                   [], [bST])
                            P.emit('dve', lambda e: e.memset(ST[0:4, 24:28], 0.0), [], [bST])
                            for h in range(4):
                                P.emit('act', lambda e, h=h: e.activation(
                                    out=Ef[0:4, h * 256:(h + 1) * 256],
                                    in_=PS[pS[h // 2]][0:4, (h % 2) * 256:(h % 2) * 256 + 256],
                                    func=AF.Exp, scale=SCALE, bias=ST[0:4, 20 + h:21 + h],
                                    accum_out=ST[0:4, 24 + h:25 + h]), [bPS[pS[h // 2]], bST], [bE, bST])
                            held.discard(pS[0])
                            held.discard(pS[1])
                            P.emit('dve', lambda e: e.reciprocal(out=ST[0:4, 28:32], in_=ST[0:4, 24:28]), [], [bST])
                            for h in range(4):
                                P.emit('dve', lambda e, h=h: e.tensor_scalar(
                                    out=Abf[0:4, h * 256:(h + 1) * 256], in0=Ef[0:4, h * 256:(h + 1) * 256],
                                    scalar1=ST[0:4, 28 + h:29 + h], scalar2=None, op0=ALU.mult), [bE, bST], [bA])
                            pi = psum()
                            for hm in range(8):
                                P.emit('pe', lambda e, pi=pi, hm=hm: e.matmul(
                                    PS[pi][:, hm * 4:(hm + 1) * 4], lhsT=Abf[0:4, hm * 128:(hm + 1) * 128],
                                    rhs=ident[0:4, 0:4], start=True, stop=True), [bA, bCONST], [bPS[pi]], sig=(hm == 7))
                            altcopy(ATb, PS[pi][:, 0:32], [bPS[pi]], [bA])
                        pO = psum()
                        for h in range(4):
                            d = dict(tag=('cv', b, h), srcs=[(cv[b, :, h * 1024:(h + 1) * 1024].rearrange(
                                "(m p) d -> p m d", p=128), (lambda stg: stg[:, :].rearrange("p (m d) -> p m d", d=1024)))],
                                cast=('plain',), np=128, ne=2048)
                            si = slab_get(d)
                            if P.plan_mode:
                                continue
                            for dc in range(8):
                                for mc in range(2):
                                    P.emit('pe', lambda e, h=h, dc=dc, mc=mc, si=si, pO=pO: e.matmul(
                                        PS[pO][:, (h * 8 + dc) * 4:(h * 8 + dc + 1) * 4],
                                        lhsT=SLAB[si][:, mc * 1024 + dc * 128:mc * 1024 + (dc + 1) * 128],
                                        rhs=ATb[:, (h * 2 + mc) * 4:(h * 2 + mc + 1) * 4], start=(mc == 0), stop=(mc == 1)),
                                        bSLAB[si] + [bA], [bPS[pO]], sig=(mc == 1 and dc == 7))
                        if not P.plan_mode:
                            altcopy(HT[:, :, q0:q0 + 4], ps3(pO, 4)[:, 0:32, :], [bPS[pO]], [bHT])
                    for b_ in range(16):
                        attn_sample_seq(b_)
                alias([bTMP], [bE, bA])

                stage(f'g{gi}p6')
                linear_B(('wo', gi), w_o, HT, bHT, 32, cb8, tiles, plain, res_cons)

                for i in range(3):
                    norm_T(bX[i], X[i], tiles[i][1], HT, bHT, tiles[i][0])

                stage(f'g{gi}p8')
                SG = TMP[:, 0:512]
                bSG = Buf("SG")
                alias([bSG], [bTMP])
                nblk = [8, 8, 8, 8, 8, 3]
                j0 = 0
                for fb, nj in enumerate(nblk):
                    pieces = [[((j0 + j) * 256, 256, 0), (DFF + (j0 + j) * 256, 256, 256)] for j in range(nj)]

                    def gu_cons(js, pis):
                        for c in range(2):
                            P.emit('act', lambda e, c=c: e.activation(out=SG[:, 0:Tg], in_=PS[pis[c]][:, 0:Tg], func=AF.Silu),
                                   [bPS[pis[c]]], [bSG])
                            P.emit('dve', lambda e, c=c: e.tensor_tensor(
                                out=B2[:, js * 2 + c, 0:Tg], in0=SG[:, 0:Tg], in1=PS[pis[c + 2]][:, 0:Tg], op=ALU.mult),
                                [bSG, bPS[pis[c + 2]]], [bB2])
                    linear_A(('wgu', gi, fb), w_gu, HT, bHT, 32, pieces, Tg, gcast(2), gu_cons)
                    linear_B(('wdn', gi, fb), w_down[j0 * 256:(j0 + nj) * 256, :], B2, bB2, nj * 2, cb8, tiles, plain,
                             res_cons)
                    j0 += nj
                alias([bTMP], [bSG])

                stage(f'g{gi}p9')
                P.dma('sp', 'misc', B2[:, :, :].rearrange("p a b -> p (a b)")[:, 0:8192].bitcast(F32), gfin_d, [], [bB2])
                GF = B2[:, :, :].rearrange("p a b -> p (a b)")[:, 0:8192].bitcast(F32)
                for i, (kind, idx) in enumerate(grp):
                    nt = tiles[i][1]
                    ss, sq, rs = ST[0:nt, 0:1], ST[0:nt, 1:2], ST[0:nt, 2:3]
                    P.emit('dve', lambda e: e.memset(ST[:, 0:1], 0.0), [], [bST])
                    P.emit('act', lambda e, i=i, nt=nt, ss=ss: e.activation(out=XN[0:nt, :], in_=X[i][0:nt, :], func=AF.Square,
                                                                     accum_out=ss), [bX[i]], [bR1, bST])
                    P.emit('act', lambda e, nt=nt, ss=ss, sq=sq: e.activation(out=sq, in_=ss, func=AF.Sqrt, scale=1.0 / D,
                                                                       bias=EPS_AP[0:nt]), [bCONST], [bST])
                    P.emit('dve', lambda e, sq=sq, rs=rs: e.reciprocal(out=rs, in_=sq), [], [bST])
                    P.emit('dve', lambda e, i=i, nt=nt, rs=rs: e.scalar_tensor_tensor(
                        out=X[i][0:nt, :], in0=X[i][0:nt, :], scalar=rs, in1=GF[0:nt, :], op0=ALU.mult, op1=ALU.mult),
                        [bST, bB2], [bX[i]])
                    dst = yp[idx * 128:(idx + 1) * 128, :] if kind == 'p' else ys
                    store(dst, X[i][0:nt, :], [bX[i]])

            for gi_, grp_ in enumerate(GROUPS):
                group_pass(gi_, grp_)


        EPS_AP = ST[:, 8:9]
        P0 = Prog(True)
        run_all(P0)
        P1 = Prog(False, P0.plan)
        run_all(P1)
        assert P1.si == len(P0.plan) or KSTOP, (P1.si, len(P0.plan))
        print("plan", len(P0.plan), {k: len(v.ops) for k, v in P1.E.items()}, {k: v.count for k, v in P1.E.items()}, flush=True)

        block = es.enter_context(nc.Block())

        def replay(engname):
            def f(e):
                for (waits, fn, upd) in P1.E[engname].ops:
                    for (k, v) in waits:
                        e.wait_ge(SEM[k], v)
                    if fn is None:
                        continue
                    ins = fn(e)
                    if upd is not None:
                        ins.then_inc(SEM[upd[0]], upd[1])
            return f

        block.tensor(replay('pe'))
        block.scalar(replay('act'))
        block.vector(replay('dve'))
        block.sync(replay('sp'))
    return nc


def _consts():
    bf = ml_dtypes.bfloat16
    cb = np.zeros((128, 2432), np.float32)
    cb[:, 0:128] = np.eye(128)

    def Bmat(a0, rows_off):
        out = np.zeros((4, 128, 128), np.float32)
        for g, w in enumerate(WINS):
            for tl in range(128):
                t = a0 + tl
                cnt = min(t + 1, w)
                for sl in range(128):
                    s = a0 + rows_off + sl
                    v = 0.0
                    if s >= 0 and t - w < s <= t:
                        v += 1.0 / cnt
                    if s == t:
                        v -= 1.0
                    out[g, sl, tl] = v
        return out
    bt0 = Bmat(0, 0)
    btm = Bmat(1024, 0)
    btp = Bmat(1024, -128)
    bsT = np.zeros((4, 64, 64), np.float32)
    bst = np.zeros((2, 4, 120, 64), np.float32)
    for g, w in enumerate(WINS):
        for b in range(16):
            for lt in range(4):
                t = b * 4 + lt
                for i in range(15 + lt - w + 1, 15 + lt + 1):
                    if i >= 15:
                        bsT[g, b * 4 + (i - 15), t] += 1.0 / w
                    else:
                        bst[b // 8, g, (b % 8) * 15 + i, t] += 1.0 / w
                bsT[g, t, t] -= 1.0
    return cb, bt0, btm, btp, bsT, bst


_CACHE = {}


def kernel(x_prompt, x_sample, cache_mem_k, cache_mem_v, state_pool, mem_prompt, g_mix, w_in, g_v, w_pool,
           pool_scale, w_s, b_s, w_out, g_xattn, g_mem, w_q, w_k, w_v, w_o, g_ffn, w_gate_up, w_down, g_final):
    f = lambda a: np.ascontiguousarray(np.asarray(a, dtype=np.float32))
    x_prompt, x_sample = f(x_prompt), f(x_sample)
    ckf = f(cache_mem_k)[0].reshape(128, NMEM, D)
    cvf = f(cache_mem_v)[0].reshape(128, NMEM, D)
    stf = f(state_pool)[0]
    memf = f(mem_prompt)
    if 'nc' not in _CACHE:
        _CACHE['nc'] = build_program()
    nc = _CACHE['nc']
    cb, bt0, btm, btp, bsT, bst = _consts()
    gcols = np.concatenate([f(g)[0].reshape(32, 128).T for g in (g_mix, g_xattn, g_ffn, g_mem)], axis=1)
    smallA = np.zeros((128, 152), np.float32)
    smallA[:, 0:128] = gcols
    smallA[:, 128:144] = f(pool_scale)[0].reshape(16, 128).T
    smallA[:, 144:148] = f(b_s)[0].T
    smallA[0:64, 148:152] = np.tile(f(b_s)[0][:, :4].T, (16, 1))
    ws = f(w_s)[0]
    smallB = np.zeros((128, 1536), np.float32)
    smallB[:, 0:512] = ws.transpose(2, 0, 1).reshape(128, 512)
    tril_T = np.triu(np.ones((128, 128), np.float32))
    smallB[:, 512:1024] = np.tile(tril_T, (1, 4))
    ws4 = ws[:, :4, :4].transpose(2, 0, 1)
    smallB[0:64, 1024:1280] = np.tile(ws4[:, :, None, :], (16, 1, 16, 1)).reshape(64, 256)
    m4 = np.triu(np.ones((4, 4), np.float32))
    smallB[0:64, 1280:1536] = np.tile(np.kron(np.eye(16, dtype=np.float32), m4)[:, None, :], (1, 4, 1)).reshape(64, 256)
    gfin_bc = np.ascontiguousarray(np.broadcast_to(f(g_final)[None, :], (128, D)))
    gv_bc = np.ascontiguousarray(np.broadcast_to(f(g_v)[0][None, :], (128, DG)))
    wd = dict(w_in=f(w_in)[0], w_pool=f(w_pool)[0].reshape(2048, 512), w_out=f(w_out)[0], w_q=f(w_q)[0],
              w_k=f(w_k)[0], w_v=f(w_v)[0], w_o=f(w_o)[0], w_gu=f(w_gate_up)[0], w_down=f(w_down)[0])
    def core_map(c):
        b, half = c // 2, c % 2
        cbc = cb.copy()
        first = bt0 if half == 0 else btm
        cbc[:, 128:640] = first.transpose(1, 0, 2).reshape(128, 512)
        cbc[:, 640:1152] = btm.transpose(1, 0, 2).reshape(128, 512)
        cbc[:, 1152:1664] = btp.transpose(1, 0, 2).reshape(128, 512)
        cbc[0:64, 1664:1920] = bsT.transpose(1, 0, 2).reshape(64, 256)
        cbc[0:120, 1920:2432] = bst.transpose(2, 0, 1, 3).reshape(120, 512)
        xh = x_prompt[b, 896:1024] if half == 1 else np.zeros((128, D), np.float32)
        m = dict(xp=x_prompt[b, half * 1024:(half + 1) * 1024], xh=np.ascontiguousarray(xh),
                 xs=x_sample[c * 16:(c + 1) * 16].reshape(64, D), memx=memf[b],
                 ck=ckf[c * 16:(c + 1) * 16], cv=cvf[c * 16:(c + 1) * 16], stp=stf[c * 16:(c + 1) * 16],
                 smallA=smallA, smallB=smallB, gfin_bc=gfin_bc, gv_bc=gv_bc, cbf=cbc.astype(ml_dtypes.bfloat16))
        m.update(wd)
        return {k: np.ascontiguousarray(v) for k, v in m.items()}
    if _CACHE.get('sim_core') is not None:
        return nc, core_map(_CACHE['sim_core'])
    if _CACHE.get('sim_cores') is not None:
        return nc, [core_map(c) for c in _CACHE['sim_cores']]
    in_maps = [core_map(c) for c in range(NCORES)]
    res = run_bass_kernel_spmd(nc, in_maps, core_ids=list(range(NCORES))).results
    y_prompt = np.zeros((4, 2048, D), np.float32)
    y_sample = np.zeros((128, 4, D), np.float32)
    mk = np.zeros((1, 4, NMEM, 4, 1024), np.float32)
    mv = np.zeros((1, 4, NMEM, 4, 1024), np.float32)
    psp = np.zeros((1, 4, 15, DP), np.float32)
    pss = np.zeros((1, 128, 15, DP), np.float32)
    gvs = np.zeros((1, 128, 4, DG), np.float32)
    for c in range(NCORES):
        b, half = c // 2, c % 2
        r = res[c]
        y_prompt[b, half * 1024:(half + 1) * 1024] = r["yp"]
        y_sample[c * 16:(c + 1) * 16] = r["ys"].reshape(16, 4, D)
        if half == 0:
            mk[0, b] = r["mk"].reshape(NMEM, 4, 1024)
            mv[0, b] = r["mv"].reshape(NMEM, 4, 1024)
        else:
            psp[0, b] = r["psp"]
        pss[0, c * 16:(c + 1) * 16] = r["pss"]
        gvs[0, c * 16:(c + 1) * 16] = r["gvs"].reshape(16, 4, DG)
    return (y_prompt, y_sample, mk, mv, psp, pss, gvs)
```

```python
import contextlib
import numpy as np
import ml_dtypes
import concourse.bass as bass
import concourse.mybir as mybir
from concourse.bass_utils import run_bass_kernel_spmd

F32, BF16 = mybir.dt.float32, mybir.dt.bfloat16
AF = mybir.ActivationFunctionType
ALU = mybir.AluOpType
AX = mybir.AxisListType

D = 4096
DP = 2048
DG = 2048
DFF = 11008
NMEM = 256
EPS = 1e-6
WINS = (2, 4, 8, 16)
SCALE = 1024 ** -0.5
NCORES = 8
GROUPS = [[('p', 0), ('p', 1), ('p', 2)], [('p', 3), ('p', 4), ('p', 5)], [('p', 6), ('p', 7), ('s', 0)]]
NSTG, NSLB = 4, 4
SE = 2048


import os
KSTOP = os.environ.get('KSTOP', '')
STOREQ = os.environ.get('STOREQ', 'act')


class StopBuild(Exception):
    pass


def stage(name):
    if KSTOP and KSTOP == name:
        raise StopBuild()


class Buf:
    def __init__(self, name, excl=False):
        self.name = name
        self.w = None
        self.r = []
        self.excl = excl


class Eng:
    def __init__(self, name, key):
        self.name, self.key = name, key
        self.count = 0
        self.seen = {}
        self.ops = []
        self.pend = [key, None]


class Prog:
    def __init__(self, plan_mode, plan=None):
        self.plan_mode = plan_mode
        self.plan = [] if plan_mode else plan
        self.E = {n: Eng(n, n) for n in ('pe', 'act', 'dve', 'sp')}
        self.dcount = {}
        self.dlast = {}
        self.si = 0
        self.n_dma = 0
        self.n_cast = 0
        self.ps_i = 0
        self.store_i = 0
        self.alt = 0

    def _waits(self, E, reads, writes, extra=()):
        evs = []
        for b in reads:
            if b.w is not None:
                evs.append(b.w)
            if b.excl:
                evs.extend(ev for ev in b.r if ev[0] != E.key)
        for b in writes:
            if b.w is not None:
                evs.append(b.w)
            evs.extend(b.r)
        evs.extend(extra)
        waits = []
        for ev in evs:
            key, val = ev
            if val is None:
                assert key == E.key == 'pe', (key, E.name)
                continue
            if key == E.key and E.name == 'pe':
                continue
            if E.seen.get(key, 0) >= val:
                continue
            E.seen[key] = val
            waits.append((key, val))
        return waits

    def _post(self, ev, reads, writes):
        for b in reads:
            b.r.append(ev)
        for b in writes:
            b.w = ev
            b.r = []

    def emit(self, eng, fn, reads=(), writes=(), sig=True):
        if self.plan_mode:
            return None
        E = self.E[eng]
        waits = self._waits(E, reads, writes)
        if sig:
            E.count += 1
            ev = [E.key, E.count]
            E.pend[1] = E.count
            E.pend = [E.key, None]
            E.ops.append((waits, fn, (E.key, 1)))
        else:
            ev = E.pend
            E.ops.append((waits, fn, None))
        self._post(ev, reads, writes)
        return ev

    def dma(self, q, semkey, out, in_, reads=(), writes=()):
        if self.plan_mode:
            return None
        E = self.E[q]
        extra = [self.dlast[semkey]] if semkey in self.dlast else []
        waits = self._waits(E, reads, writes, extra)
        self.dcount[semkey] = self.dcount.get(semkey, 0) + 16
        ev = [semkey, self.dcount[semkey]]
        self.dlast[semkey] = ev
        E.ops.append((waits, (lambda e, o=out, i=in_: e.dma_start(out=o, in_=i)), (semkey, 16)))
        self._post(ev, reads, writes)
        return ev


def alias(new_bufs, old_bufs):
    evs = []
    for b in old_bufs:
        if b.w is not None:
            evs.append(b.w)
        evs.extend(b.r)
    for b in new_bufs:
        b.r.extend(evs)


def build_program():
    nc = bass.Bass("TRN2", target_bir_lowering=False)

    def din(name, shape, dt=F32):
        return nc.dram_tensor(name, list(shape), dt, kind="ExternalInput").ap()

    def dout(name, shape):
        return nc.dram_tensor(name, list(shape), F32, kind="ExternalOutput").ap()

    xp, xh, xs = din("xp", [1024, D]), din("xh", [128, D]), din("xs", [64, D])
    memx = din("memx", [NMEM, D])
    ck, cv = din("ck", [16, NMEM, D]), din("cv", [16, NMEM, D])
    stp = din("stp", [16, 15, DP])
    w_in, w_pool = din("w_in", [D, 6144]), din("w_pool", [2048, 512])
    w_out, w_q, w_k, w_v, w_o = (din(n, [D, D]) for n in ("w_out", "w_q", "w_k", "w_v", "w_o"))
    w_gu, w_down = din("w_gu", [D, 2 * DFF]), din("w_down", [DFF, D])
    smallA_d, smallB_d = din("smallA", [128, 152]), din("smallB", [128, 1536])
    gfin_d, gv_d = din("gfin_bc", [128, D]), din("gv_bc", [128, DG])
    cbf_d = din("cbf", [128, 2432], BF16)
    yp, ys = dout("yp", [1024, D]), dout("ys", [64, D])
    mk_o, mv_o = dout("mk", [NMEM, D]), dout("mv", [NMEM, D])
    psp_o, pss_o, gvs_o = dout("psp", [15, DP]), dout("pss", [16, 15, DP]), dout("gvs", [64, DG])
    kvs = nc.dram_tensor("kvscratch", [128, 16384], BF16, kind="Internal").ap()

    es = contextlib.ExitStack()
    with es:
        def sb(name, shape, dt):
            return es.enter_context(nc.sbuf_tensor(name, list(shape), dt))

        X = [sb(f"X{i}", [128, D], F32) for i in range(3)]
        HT = sb("HT", [128, 32, 384], BF16)
        B2 = sb("B2", [128, 32, 384], BF16)
        SLAB = [sb(f"SLAB{i}", [128, SE], BF16) for i in range(NSLB)]
        STG = [sb(f"STG{i}", [128, SE], F32) for i in range(NSTG)]
        R1 = sb("R1", [128, 18432], BF16)
        PPREV = sb("PPREV", [128, DP], BF16)
        R3 = sb("R3", [128, 2048], F32)
        TMP = sb("TMP", [128, 2048], F32)
        SMA = sb("SMA", [128, 152], F32)
        CBF = sb("CBF", [128, 2432], BF16)
        WST = sb("WST", [128, 512], BF16)
        WSTS = sb("WSTS", [128, 256], BF16)
        ST = sb("ST", [128, 64], F32)
        PS = [es.enter_context(nc.psum_tensor(f"PS{i}", [128, 512], F32)) for i in range(8)]
        sem_names = ['pe', 'act', 'dve', 'sp'] + [f"w{i}" for i in range(8)] + [f"v{i}" for i in range(8)] + [f"x{i}" for i in range(3)] + \
                    [f"s{i}" for i in range(4)] + ['misc']
        SEM = {n: es.enter_context(nc.semaphore(n)) for n in sem_names}

        HTf = HT[:, :, :].rearrange("p a b -> p (a b)")
        R3bf = R3[:, :].bitcast(BF16)
        TMPbf = TMP[:, 1024:2048].bitcast(BF16)
        ident = CBF[:, 0:128]
        gcol = lambda gi, k0, nk: SMA[:, gi * 32 + k0: gi * 32 + k0 + nk]
        pscol = lambda c: SMA[:, 128 + c:128 + c + 1]
        P_ = [R1[:, i * 2048:(i + 1) * 2048] for i in range(3)]
        U_ = [R1[:, 6144 + i * 2048: 6144 + (i + 1) * 2048] for i in range(3)]
        V_ = [R1[:, 12288 + i * 2048: 12288 + (i + 1) * 2048] for i in range(3)]
        XN = R1[:, 12288:12288 + 4096]
        KTm = R1[:, 0:8192].rearrange("p (a b) -> p a b", b=256)
        Vm = R1[:, 8192:16384].rearrange("p (a b) -> p a b", b=4096)
        KTb = R1[:, 16384:18432].rearrange("p (a b) -> p a b", b=256)
        ATp = R3bf[:, 0:3072].rearrange("p (a b) -> p a b", b=384)
        pooledT = HTf[:, 0:6144].rearrange("p (a b) -> p a b", b=384)
        Gt = HTf[:, 6144:8192]
        Ef = TMP[:, 0:1024]
        Abf = TMPbf[:, 0:1024]
        ATb = TMPbf[:, 1024:1056]

        def ps3(i, b=128):
            return PS[i][:, :].rearrange("p (a b) -> p a b", b=b)

        def run_all(P):
            try:
                run_all_inner(P)
            except StopBuild:
                pass
            if not P.plan_mode:
                E = P.E[STOREQ]
                waits = []
                for k, ev in P.dlast.items():
                    if k.startswith('s') and k != 'sp':
                        waits.append((k, ev[1]))
                E.ops.append((waits, None, None))

        def run_all_inner(P):
            bX = [Buf(f"X{i}") for i in range(3)]
            bHT = Buf("HT")
            bB2 = Buf("B2")
            bSLAB = [[Buf(f"SLAB{i}a"), Buf(f"SLAB{i}b"), Buf(f"SLAB{i}c")] for i in range(NSLB)]
            bSTG = [[Buf(f"STG{i}a"), Buf(f"STG{i}b")] for i in range(NSTG)]
            bR1 = Buf("R1")
            bPP = Buf("PPREV")
            bR3 = Buf("R3")
            bTMP = Buf("TMP")
            bCONST = Buf("CONST")
            bST = Buf("ST")
            bPS = [Buf(f"PS{i}", excl=True) for i in range(8)]
            bKVS = Buf("KVS")
            bOUT = Buf("OUT")

            held = set()

            def psum(hold=False):
                while True:
                    i = P.ps_i % 8
                    P.ps_i += 1
                    if i not in held:
                        break
                if hold:
                    held.add(i)
                return i

            def altcopy(out, in_, reads, writes):
                P.alt += 1
                if P.alt % 2:
                    return P.emit('act', lambda e: e.activation(out=out, in_=in_, func=AF.Copy), reads, writes)
                return P.emit('dve', lambda e: e.tensor_copy(out=out, in_=in_), reads, writes)

            def store(out, in_, reads):
                if os.environ.get('NOSTORE'):
                    return None
                k = f"s{P.store_i % 4}"
                P.store_i += 1
                return P.dma(STOREQ, k, out, in_, reads=reads, writes=[bOUT])

            def slab_get(desc, keep=0):
                if P.plan_mode:
                    P.plan.append(desc)
                    return 0
                i = P.si
                P.si += 1
                assert P.plan[i]['tag'] == desc['tag'], (P.plan[i]['tag'], desc['tag'])
                n = len(P.plan)

                def do_dma(j):
                    d = P.plan[j]
                    s = j % NSTG
                    nsrc = len(d['srcs'])
                    for pi_, (src, dstf) in enumerate(d['srcs']):
                        wr = bSTG[s] if nsrc == 1 else [bSTG[s][pi_]]
                        P.dma('sp', f"{'wv'[pi_]}{j % 8}", dstf(STG[s]), src, reads=[], writes=wr)

                def do_cast(j):
                    d = P.plan[j]
                    s, b = j % NSTG, j % NSLB
                    np_, ne = d['np'], d['ne']
                    c = d['cast']
                    if c[0] == 'plain':
                        h_ = (ne // 2) if ne >= 1024 else ne
                        P.emit('act', lambda e: e.activation(out=SLAB[b][0:np_, 0:h_], in_=STG[s][0:np_, 0:h_],
                                                             func=AF.Copy), bSTG[s], [bSLAB[b][0]])
                        if h_ < ne:
                            P.emit('dve', lambda e: e.tensor_copy(out=SLAB[b][0:np_, h_:ne], in_=STG[s][0:np_, h_:ne]),
                                   bSTG[s], [bSLAB[b][1], bSLAB[b][2]])
                    else:
                        _, gi, k0, nk = c
                        assert nk == 4
                        o3 = SLAB[b][:, 0:1024].rearrange("p (k n) -> p k n", n=512)
                        i3 = STG[s][:, 0:1024].rearrange("p (k n) -> p k n", n=512)
                        g3 = gcol(gi, k0, 2).unsqueeze(2).broadcast_to([128, 2, 512])
                        P.emit('dve', lambda e: e.tensor_tensor(out=o3, in0=i3, in1=g3, op=ALU.mult),
                               bSTG[s] + [bCONST], [bSLAB[b][0]])
                        for k_ in (2, 3):
                            P.emit('act', lambda e, k_=k_: e.activation(
                                out=SLAB[b][:, k_ * 512:(k_ + 1) * 512], in_=STG[s][:, k_ * 512:(k_ + 1) * 512],
                                func=AF.Copy, scale=gcol(gi, k0 + k_, 1)), bSTG[s] + [bCONST], [bSLAB[b][k_ - 1]])

                while P.n_cast < min(i + NSLB - keep, n):
                    j = P.n_cast
                    while P.n_dma <= j:
                        do_dma(P.n_dma)
                        P.n_dma += 1
                    do_cast(j)
                    P.n_cast += 1
                while P.n_dma < min(P.n_cast + NSTG, n):
                    do_dma(P.n_dma)
                    P.n_dma += 1
                return i % NSLB

            def wdesc(tag, W, r0, nk, pieces, cast):
                srcs = []
                for (c0, ncol, d0) in pieces:
                    src = W[r0:r0 + nk * 128, c0:c0 + ncol].rearrange("(k p) n -> p k n", p=128)
                    srcs.append((src, (lambda stg, d0=d0, ncol=ncol, nk=nk:
                                       stg[:, 0:nk * 512].rearrange("p (k n) -> p k n", n=512)[:, :, d0:d0 + ncol])))
                return dict(tag=tag, srcs=srcs, cast=cast, np=128, ne=nk * 512)

            def norm_T(xb, xt, ntok, dst3, dstb, col0):
                ss, sq, rs = ST[0:ntok, 0:1], ST[0:ntok, 1:2], ST[0:ntok, 2:3]
                P.emit('dve', lambda e: e.memset(ST[:, 0:1], 0.0), [], [bST])
                P.emit('act', lambda e: e.activation(out=XN[0:ntok, :], in_=xt[0:ntok, :], func=AF.Square,
                                                     accum_out=ss), [xb], [bR1, bST])
                P.emit('act', lambda e: e.activation(out=sq, in_=ss, func=AF.Sqrt, scale=1.0 / D, bias=EPS_AP[0:ntok]),
                       [bCONST], [bST])
                P.emit('dve', lambda e: e.reciprocal(out=rs, in_=sq), [], [bST])
                P.emit('act', lambda e: e.activation(out=XN[0:ntok, :], in_=xt[0:ntok, :], func=AF.Copy, scale=rs),
                       [xb, bST], [bR1])
                for kg in range(8):
                    pi = psum()
                    for i in range(4):
                        c = kg * 4 + i
                        P.emit('pe', lambda e, c=c, i=i, pi=pi: e.matmul(
                            PS[pi][:, i * 128:i * 128 + ntok], lhsT=XN[0:ntok, c * 128:(c + 1) * 128],
                            rhs=ident[0:ntok, 0:ntok], start=True, stop=True),
                            [bR1, bCONST], [bPS[pi]], sig=(i == 3))
                    altcopy(dst3[:, kg * 4:kg * 4 + 4, col0:col0 + ntok], ps3(pi)[:, :, 0:ntok], [bPS[pi]], [dstb])

            def linear_B(tag, W, inT, inb, nk_tot, col_blocks, tiles, cast_fn, consumer):
                nq = (nk_tot + 3) // 4
                for jb, (c0, ncol) in enumerate(col_blocks):
                    pis = [psum() for _ in tiles]
                    for q in range(nq):
                        nk = min(4, nk_tot - q * 4)
                        si = slab_get(wdesc((tag, jb, q), W, q * 512, nk, [(c0, ncol, 0)], cast_fn(q * 4, nk)))
                        if P.plan_mode:
                            continue
                        for ti, (tc0, ntok) in enumerate(tiles):
                            for k in range(nk):
                                kk = q * 4 + k
                                last = (kk == nk_tot - 1)
                                P.emit('pe', lambda e, pi=pis[ti], kk=kk, k=k, si=si, tc0=tc0, ntok=ntok, last=last, ncol=ncol:
                                       e.matmul(PS[pi][0:ntok, 0:ncol], lhsT=inT[:, kk, tc0:tc0 + ntok],
                                                rhs=SLAB[si][:, k * 512:k * 512 + ncol], start=(kk == 0), stop=last),
                                       [inb] + bSLAB[si], [bPS[pis[ti]]],
                                       sig=(k == nk - 1 and (last or ti == len(tiles) - 1)))
                    if P.plan_mode:
                        continue
                    for ti in range(len(tiles)):
                        consumer(jb, ti, pis[ti])

            def linear_A(tag, W, inT, inb, nk_tot, slab_cols, Tg, cast_fn, consumer):
                nq = nk_tot // 4
                for js, pieces in enumerate(slab_cols):
                    pis = [psum() for _ in range(4)]
                    for q in range(nq):
                        si = slab_get(wdesc((tag, js, q), W, q * 512, 4, pieces, cast_fn(q * 4, 4)))
                        if P.plan_mode:
                            continue
                        for c in range(4):
                            for k in range(4):
                                kk = q * 4 + k
                                last = (kk == nk_tot - 1)
                                P.emit('pe', lambda e, pi=pis[c], kk=kk, k=k, si=si, c=c, last=last:
                                       e.matmul(PS[pi][:, 0:Tg], lhsT=SLAB[si][:, k * 512 + c * 128:k * 512 + (c + 1) * 128],
                                                rhs=inT[:, kk, 0:Tg], start=(kk == 0), stop=last),
                                       [inb] + bSLAB[si], [bPS[pis[c]]],
                                       sig=(k == 3 and (last or c == 3)))
                    if P.plan_mode:
                        continue
                    consumer(js, pis)

            plain = lambda k0, nk: ('plain',)
            gcast = lambda gi: (lambda k0, nk: ('g', gi, k0, nk))

            P.dma('sp', 'misc', SMA[:, :], smallA_d, [], [bCONST])
            P.dma('sp', 'misc', CBF[:, :], cbf_d, [], [bCONST])
            P.dma('sp', 'misc', TMP[:, 0:1536], smallB_d, [], [bTMP])
            P.emit('dve', lambda e: e.memset(ST[:, 8:9], EPS), [], [bCONST])
            P.emit('dve', lambda e: e.tensor_tensor(out=WST[:, :], in0=TMP[:, 0:512], in1=TMP[:, 512:1024], op=ALU.mult),
                   [bTMP], [bCONST])
            P.emit('dve', lambda e: e.tensor_tensor(out=WSTS[0:64, :], in0=TMP[0:64, 1024:1280],
                                                    in1=TMP[0:64, 1280:1536], op=ALU.mult), [bTMP], [bCONST])

            stage('setup')
            P.dma('sp', 'x0', X[0][:, :], xh, [], [bX[0]])
            norm_T(bX[0], X[0], 128, HT, bHT, 0)

            def halo_cons(jb, ti, pi):
                altcopy(PPREV[:, jb * 512:(jb + 1) * 512], PS[pi][:, :], [bPS[pi]], [bPP])
            linear_B('halo', w_in, HT, bHT, 32, [(j * 512, 512) for j in range(4)], [(0, 128)], gcast(0), halo_cons)

            stage('halo')
            for i in range(2):
                P.dma('sp', f"x{i}", X[i][:, :], memx[i * 128:(i + 1) * 128, :], [], [bX[i]])
            for i in range(2):
                norm_T(bX[i], X[i], 128, HT, bHT, i * 128)
            stage('mem_norm')
            OF = [TMP[:, 0:512], TMP[:, 512:1024]]
            KB = TMPbf[:, 0:512]
            bOF = [Buf("OF0"), Buf("OF1")]
            bKB = Buf("KB")
            alias(bOF + [bKB], [bTMP])
            ofi = [0]

            def k_cons(jb, ti, pi):
                if os.environ.get('MEMVAR') == 'trivial':
                    altcopy(Vm[:, ti, jb * 512:(jb + 1) * 512], PS[pi][:, :], [bPS[pi]], [bR1])
                    return
                o = ofi[0] % 2
                ofi[0] += 1
                mvv = os.environ.get('MEMVAR')
                if mvv != 'b':
                    P.emit('act', lambda e: e.activation(out=OF[o], in_=PS[pi][:, :], func=AF.Copy), [bPS[pi]], [bOF[o]])
                P.emit('dve', lambda e: e.tensor_copy(out=KB, in_=PS[pi][:, :]), [bPS[pi]], [bKB])
                store(mk_o[ti * 128:(ti + 1) * 128, jb * 512:(jb + 1) * 512], OF[o], [bOF[o]])
                if mvv == 'a':
                    return
                p2 = psum()
                for c in range(4):
                    P.emit('pe', lambda e, c=c, p2=p2: e.matmul(PS[p2][:, c * 128:(c + 1) * 128],
                                                                lhsT=KB[:, c * 128:(c + 1) * 128], rhs=ident,
                                                                start=True, stop=True),
                           [bKB, bCONST], [bPS[p2]], sig=(c == 3))
                altcopy(KTm[:, jb * 4:jb * 4 + 4, ti * 128:(ti + 1) * 128], ps3(p2), [bPS[p2]], [bR1])

            def v_cons(jb, ti, pi):
                o = ofi[0] % 2
                ofi[0] += 1
                P.emit('act', lambda e: e.activation(out=OF[o], in_=PS[pi][:, :], func=AF.Copy), [bPS[pi]], [bOF[o]])
                P.emit('dve', lambda e: e.tensor_copy(out=Vm[:, ti, jb * 512:(jb + 1) * 512], in_=PS[pi][:, :]),
                       [bPS[pi]], [bR1])
                store(mv_o[ti * 128:(ti + 1) * 128, jb * 512:(jb + 1) * 512], OF[o], [bOF[o]])

            cb8 = [(j * 512, 512) for j in range(8)]
            mt = [(0, 128), (128, 128)]
            linear_B('wk', w_k, HT, bHT, 32, cb8, mt, gcast(3), k_cons)
            stage('mem_k')
            linear_B('wv', w_v, HT, bHT, 32, cb8, mt, gcast(3), v_cons)
            if not os.environ.get('NOKVS'):
                P.dma(STOREQ, 'misc', kvs, R1[:, 0:16384], [bR1], [bKVS])
            alias([bTMP], bOF + [bKB])

            stage('mem')
            def group_pass(gi, grp):
                tiles = []
                for i, (kind, idx) in enumerate(grp):
                    tiles.append((i * 128, 128 if kind == 'p' else 64))
                Tg = tiles[-1][0] + tiles[-1][1]
                has_s = grp[-1][0] == 's'
                ptiles = [t for t, g in zip(tiles, grp) if g[0] == 'p']
                Tp = len(ptiles) * 128

                for i, (kind, idx) in enumerate(grp):
                    nt = tiles[i][1]
                    src = xp[idx * 128:(idx + 1) * 128, :] if kind == 'p' else xs
                    P.dma('sp', f"x{i}", X[i][0:nt, :], src, [], [bX[i]])
                P.dma('sp', 'misc', R3[:, :], gv_d, [], [bR3])
                for i in range(3):
                    norm_T(bX[i], X[i], tiles[i][1], HT, bHT, tiles[i][0])

                stage(f'g{gi}p0')
                ZV = [TMP[:, 0:512], TMP[:, 512:1024]]
                PF = [TMP[:, 1024:1536], TMP[:, 1536:2048]]
                bZV = [Buf("ZV0"), Buf("ZV1")]
                bPF = [Buf("PF0"), Buf("PF1")]
                alias(bZV + bPF, [bTMP])
                zi = [0]
                pfi = [0]

                def in_cons(jb, ti, pi, grp=grp, tiles=tiles, bZV=bZV, bPF=bPF, ZV=ZV, PF=PF, zi=zi, pfi=pfi):
                    kind, idx = grp[ti]
                    nt = tiles[ti][1]
                    if jb < 4:
                        P.emit('act', lambda e: e.activation(out=P_[ti][0:nt, jb * 512:(jb + 1) * 512],
                                                             in_=PS[pi][0:nt, :], func=AF.Copy), [bPS[pi]], [bR1])
                        if kind == 's' or idx == 7:
                            o = pfi[0] % 2
                            pfi[0] += 1
                            P.emit('dve', lambda e: e.tensor_copy(out=PF[o][0:nt, :], in_=PS[pi][0:nt, :]),
                                   [bPS[pi]], [bPF[o]])
                            if kind == 's':
                                for b in range(16):
                                    store(pss_o[b, 11:15, jb * 512:(jb + 1) * 512], PF[o][b * 4:(b + 1) * 4, :], [bPF[o]])
                            else:
                                store(psp_o[:, jb * 512:(jb + 1) * 512], PF[o][113:128, :], [bPF[o]])
                    elif jb < 8:
                        h = jb - 4
                        P.emit('act', lambda e: e.activation(out=U_[ti][0:nt, h * 512:(h + 1) * 512],
                                                             in_=PS[pi][0:nt, :], func=AF.Gelu), [bPS[pi]], [bR1])
                    else:
                        h = jb - 8
                        o = zi[0] % 2
                        zi[0] += 1
                        z = ZV[o]
                        ss, sq, rs = ST[0:nt, 3:4], ST[0:nt, 4:5], ST[0:nt, 5:6]
                        P.emit('act', lambda e: e.activation(out=z[0:nt, :], in_=PS[pi][0:nt, :], func=AF.Gelu),
                               [bPS[pi]], [bZV[o]])
                        P.emit('dve', lambda e: e.memset(ST[:, 3:4], 0.0), [], [bST])
                        P.emit('dve', lambda e: e.tensor_tensor_scan(
                            out=PS[pi][0:nt, :], data0=z[0:nt, :], data1=z[0:nt, :], initial=0.0,
                            op0=ALU.mult, op1=ALU.add) if False else e.scalar_tensor_tensor(
                            out=PS[pi][0:nt, :], in0=z[0:nt, :], scalar=1.0, in1=z[0:nt, :], op0=ALU.mult,
                            op1=ALU.mult, accum_out=ss), [bZV[o]], [bPS[pi], bST])
                        P.emit('act', lambda e: e.activation(out=sq, in_=ss, func=AF.Sqrt, scale=1.0 / 512,
                                                             bias=EPS_AP[0:nt]), [bCONST], [bST])
                        P.emit('dve', lambda e: e.reciprocal(out=rs, in_=sq), [], [bST])
                        if kind == 's':
                            P.emit('dve', lambda e: e.scalar_tensor_tensor(
                                out=z[0:nt, :], in0=z[0:nt, :], scalar=rs, in1=R3[0:nt, h * 512:(h + 1) * 512],
                                op0=ALU.mult, op1=ALU.mult), [bST, bR3], [bZV[o]])
                            P.emit('dve', lambda e: e.tensor_copy(out=V_[ti][0:nt, h * 512:(h + 1) * 512], in_=z[0:nt, :]),
                                   [bZV[o]], [bR1])
                            store(gvs_o[:, h * 512:(h + 1) * 512], z[0:nt, :], [bZV[o]])
                        else:
                            P.emit('dve', lambda e: e.scalar_tensor_tensor(
                                out=V_[ti][0:nt, h * 512:(h + 1) * 512], in0=z[0:nt, :], scalar=rs,
                                in1=R3[0:nt, h * 512:(h + 1) * 512], op0=ALU.mult, op1=ALU.mult),
                                [bZV[o], bST, bR3], [bR1])

                linear_B(('win', gi), w_in, HT, bHT, 32, [(j * 512, 512) for j in range(12)], tiles, gcast(0), in_cons)
                alias([bTMP], bZV + bPF)

                stage(f'g{gi}p1')
                for g in range(4):
                    pp = []
                    for i, (kind, idx) in enumerate(grp):
                        nt = tiles[i][1]
                        tc0 = tiles[i][0]
                        pi = psum()
                        sst = []
                        if kind == 's':
                            for half in range(2):
                                d = dict(tag=('st', g, half),
                                         srcs=[(stp[half * 8:(half + 1) * 8].rearrange("b r c -> (b r) c")[:, g * 512:(g + 1) * 512],
                                                (lambda stg: stg[0:120, 0:512]))], cast=('plain',), np=120, ne=512)
                                sst.append(slab_get(d, keep=half))
                        if P.plan_mode:
                            continue
                        for cc in range(4):
                            osl = PS[pi][:, cc * 128:cc * 128 + nt]
                            pc = slice(g * 512 + cc * 128, g * 512 + (cc + 1) * 128)
                            if kind == 'p':
                                btc = CBF[:, 128 + g * 128:128 + (g + 1) * 128] if idx == 0 else \
                                    CBF[:, 640 + g * 128:640 + (g + 1) * 128]
                                prev = PPREV if i == 0 else P_[i - 1]
                                prevb = bPP if i == 0 else bR1
                                P.emit('pe', lambda e, osl=osl, pc=pc, btc=btc, i=i: e.matmul(
                                    osl, lhsT=P_[i][:, pc], rhs=btc, start=True, stop=False),
                                    [bR1, bCONST], [bPS[pi]], sig=False)
                                P.emit('pe', lambda e, osl=osl, pc=pc, prev=prev, g=g: e.matmul(
                                    osl, lhsT=prev[:, pc], rhs=CBF[:, 1152 + g * 128:1152 + (g + 1) * 128],
                                    start=False, stop=True), [prevb, bCONST], [bPS[pi]], sig=(cc == 3))
                            else:
                                P.emit('pe', lambda e, osl=osl, pc=pc, i=i, g=g: e.matmul(
                                    osl, lhsT=P_[i][0:64, pc], rhs=CBF[0:64, 1664 + g * 64:1664 + (g + 1) * 64],
                                    start=True, stop=False), [bR1, bCONST], [bPS[pi]], sig=False)
                                for half in range(2):
                                    P.emit('pe', lambda e, osl=osl, cc=cc, half=half, g=g, s_=sst[half]: e.matmul(
                                        osl, lhsT=SLAB[s_][0:120, cc * 128:(cc + 1) * 128],
                                        rhs=CBF[0:120, 1920 + half * 256 + g * 64:1920 + half * 256 + (g + 1) * 64],
                                        start=False, stop=(half == 1)), bSLAB[sst[half]] + [bCONST], [bPS[pi]],
                                        sig=(half == 1))
                        altcopy(pooledT[:, g * 4:(g + 1) * 4, tc0:tc0 + nt], ps3(pi)[:, :, 0:nt], [bPS[pi]], [bHT])
                    si = slab_get(wdesc(('wpool', gi, g), w_pool, g * 512, 4, [(0, 512, 0)], ('plain',)))
                    if P.plan_mode:
                        continue
                    for dc in range(4):
                        pi = psum()
                        for cc in range(4):
                            P.emit('pe', lambda e, pi=pi, cc=cc, dc=dc, si=si, g=g: e.matmul(
                                PS[pi][:, 0:Tg], lhsT=SLAB[si][:, cc * 512 + dc * 128:cc * 512 + (dc + 1) * 128],
                                rhs=pooledT[:, g * 4 + cc, 0:Tg], start=(cc == 0), stop=(cc == 3)),
                                bSLAB[si] + [bHT], [bPS[pi]], sig=(cc == 3))
                        P.emit('act', lambda e, pi=pi, c=g * 4 + dc: e.activation(
                            out=B2[:, c, 0:Tg], in_=PS[pi][:, 0:Tg], func=AF.Copy, scale=pscol(c)),
                            [bPS[pi], bCONST], [bB2])
                if has_s and not P.plan_mode:
                    for b in range(16):
                        store(pss_o[b, 0:11, :], stp[b, 4:15, :], [])
                if gi < 2:
                    P.emit('dve', lambda e: e.tensor_copy(out=PPREV[:, :], in_=P_[2]), [bR1], [bPP])
                for i, (kind, idx) in enumerate(grp):
                    nt, tc0 = tiles[i][1], tiles[i][0]
                    if P.plan_mode:
                        continue
                    for h in range(4):
                        pi = psum()
                        if kind == 'p':
                            wst, bc = WST[:, h * 128:(h + 1) * 128], SMA[:, 144 + h:145 + h]
                        else:
                            wst, bc = WSTS[0:64, h * 64:(h + 1) * 64], SMA[0:64, 148 + h:149 + h]
                        P.emit('pe', lambda e, pi=pi, wst=wst, i=i, h=h, nt=nt: e.matmul(
                            PS[pi][0:nt, :], lhsT=wst, rhs=V_[i][0:nt, h * 512:(h + 1) * 512], start=True, stop=True),
                            [bCONST, bR1], [bPS[pi]])
                        P.emit('dve', lambda e, pi=pi, bc=bc, i=i, h=h, nt=nt: e.scalar_tensor_tensor(
                            out=Gt[0:nt, h * 512:(h + 1) * 512], in0=PS[pi][0:nt, :], scalar=bc,
                            in1=U_[i][0:nt, h * 512:(h + 1) * 512], op0=ALU.add, op1=ALU.mult),
                            [bPS[pi], bCONST, bR1], [bHT])
                    for h in range(4):
                        pi = psum()
                        for cc in range(4):
                            P.emit('pe', lambda e, pi=pi, cc=cc, h=h, nt=nt: e.matmul(
                                PS[pi][:, cc * 128:cc * 128 + nt],
                                lhsT=Gt[0:nt, h * 512 + cc * 128:h * 512 + (cc + 1) * 128], rhs=ident[0:nt, 0:nt],
                                start=True, stop=True), [bHT, bCONST], [bPS[pi]], sig=(cc == 3))
                        altcopy(B2[:, 16 + h * 4:16 + (h + 1) * 4, tc0:tc0 + nt], ps3(pi)[:, :, 0:nt], [bPS[pi]], [bB2])

                stage(f'g{gi}p2')
                def res_cons(jb, ti, pi, tiles=tiles):
                    nt = tiles[ti][1]
                    xs_ = X[ti][0:nt, jb * 512:(jb + 1) * 512]
                    P.emit('dve', lambda e: e.tensor_tensor(out=xs_, in0=PS[pi][0:nt, :], in1=xs_, op=ALU.add),
                           [bPS[pi]], [bX[ti]])
                linear_B(('wout', gi), w_out, B2, bB2, 32, cb8, tiles, plain, res_cons)

                stage(f'g{gi}p3')
                for i in range(3):
                    norm_T(bX[i], X[i], tiles[i][1], HT, bHT, tiles[i][0])

                def q_cons(js, pis):
                    for c in range(4):
                        altcopy(B2[:, js * 4 + c, 0:Tg], PS[pis[c]][:, 0:Tg], [bPS[pis[c]]], [bB2])
                linear_A(('wq', gi), w_q, HT, bHT, 32, [[(j * 512, 512, 0)] for j in range(8)], Tg, gcast(1), q_cons)

                stage(f'g{gi}p5')
                P.dma('sp', 'misc', R1[:, 0:16384], kvs, [bKVS], [bR1])
                bE, bA = Buf("E"), Buf("A")
                alias([bE, bA], [bTMP])
                def attn_prompt_tile(i):
                    tc0 = tiles[i][0]
                    pA, pB = psum(), psum()
                    pp2 = [pA, pB]
                    for h in range(4):
                        for kc in range(8):
                            P.emit('pe', lambda e, h=h, kc=kc, tc0=tc0, pq=pp2[h // 2]: e.matmul(
                                PS[pq][:, (h % 2) * 256:(h % 2) * 256 + 256], lhsT=B2[:, h * 8 + kc, tc0:tc0 + 128],
                                rhs=KTm[:, h * 8 + kc, :], start=(kc == 0), stop=(kc == 7)),
                                [bB2, bR1], [bPS[pp2[h // 2]]], sig=(kc == 7))
                    mx, nm, sm, rsm = ST[:, 16:20], ST[:, 20:24], ST[:, 24:28], ST[:, 28:32]
                    for hp in range(2):
                        P.emit('dve', lambda e, hp=hp: e.tensor_reduce(
                            out=ST[:, 16 + hp * 2:18 + hp * 2], in_=ps3(pp2[hp], 256), axis=AX.X, op=ALU.max),
                            [bPS[pp2[hp]]], [bST])
                    P.emit('dve', lambda e: e.tensor_scalar(out=nm, in0=mx, scalar1=-SCALE, scalar2=None, op0=ALU.mult),
                           [], [bST])
                    P.emit('dve', lambda e: e.memset(sm, 0.0), [], [bST])
                    for h in range(4):
                        P.emit('act', lambda e, h=h: e.activation(
                            out=Ef[:, h * 256:(h + 1) * 256], in_=PS[pp2[h // 2]][:, (h % 2) * 256:(h % 2) * 256 + 256],
                            func=AF.Exp, scale=SCALE, bias=ST[:, 20 + h:21 + h], accum_out=ST[:, 24 + h:25 + h]),
                            [bPS[pp2[h // 2]], bST], [bE, bST])
                    P.emit('dve', lambda e: e.reciprocal(out=rsm, in_=sm), [], [bST])
                    for h in range(4):
                        P.emit('dve', lambda e, h=h: e.tensor_scalar(
                            out=Abf[:, h * 256:(h + 1) * 256], in0=Ef[:, h * 256:(h + 1) * 256],
                            scalar1=ST[:, 28 + h:29 + h], scalar2=None, op0=ALU.mult), [bE, bST], [bA])
                    for half in range(2):
                        pi = psum()
                        for j in range(4):
                            hm = half * 4 + j
                            P.emit('pe', lambda e, pi=pi, j=j, hm=hm: e.matmul(
                                PS[pi][:, j * 128:(j + 1) * 128], lhsT=Abf[:, hm * 128:(hm + 1) * 128], rhs=ident,
                                start=True, stop=True), [bA, bCONST], [bPS[pi]], sig=(j == 3))
                        altcopy(ATp[:, half * 4:half * 4 + 4, tc0:tc0 + 128], ps3(pi), [bPS[pi]], [bR3])
                for i, (kind, idx) in enumerate(grp):
                    if kind == 'p' and not P.plan_mode:
                        attn_prompt_tile(i)
                if not P.plan_mode:
                    for c in range(32):
                        h = c // 8
                        pi = psum()
                        for mc in range(2):
                            P.emit('pe', lambda e, pi=pi, mc=mc, c=c, h=h: e.matmul(
                                PS[pi][:, 0:Tp], lhsT=Vm[:, mc, c * 128:(c + 1) * 128], rhs=ATp[:, h * 2 + mc, 0:Tp],
                                start=(mc == 0), stop=(mc == 1)), [bR1, bR3], [bPS[pi]], sig=(mc == 1))
                        altcopy(HT[:, c, 0:Tp], PS[pi][:, 0:Tp], [bPS[pi]], [bHT])
                if has_s:
                    sc0 = tiles[-1][0]
                    def attn_sample_seq(b):
                        q0 = sc0 + 4 * b
                        pS = [psum(True), psum(True)]
                        for h in range(4):
                            d = dict(tag=('ck', b, h), srcs=[(ck[b, :, h * 1024:(h + 1) * 1024].rearrange(
                                "(m p) d -> p m d", p=128), (lambda stg: stg[:, :].rearrange("p (m d) -> p m d", d=1024)))],
                                cast=('plain',), np=128, ne=2048)
                            si = slab_get(d)
                            if P.plan_mode:
                                continue
                            for dp in range(4):
                                pi = psum()
                                for dd in range(2):
                                    dc = dp * 2 + dd
                                    for mc in range(2):
                                        P.emit('pe', lambda e, pi=pi, dd=dd, mc=mc, dc=dc, si=si: e.matmul(
                                            PS[pi][:, dd * 256 + mc * 128:dd * 256 + (mc + 1) * 128],
                                            lhsT=SLAB[si][:, mc * 1024 + dc * 128:mc * 1024 + (dc + 1) * 128], rhs=ident,
                                            start=True, stop=True), bSLAB[si] + [bCONST], [bPS[pi]],
                                            sig=(dd == 1 and mc == 1))
                                altcopy(KTb[:, dp * 2:dp * 2 + 2, :], ps3(pi, 256), [bPS[pi]], [bR1])
                            for kc in range(8):
                                P.emit('pe', lambda e, h=h, kc=kc, q0=q0, pq=pS[h // 2]: e.matmul(
                                    PS[pq][0:4, (h % 2) * 256:(h % 2) * 256 + 256], lhsT=B2[:, h * 8 + kc, q0:q0 + 4],
                                    rhs=KTb[:, kc, :], start=(kc == 0), stop=(kc == 7)),
                                    [bB2, bR1], [bPS[pS[h // 2]]], sig=(kc == 7))
                        if P.plan_mode:
                            held.discard(pS[0])
                            held.discard(pS[1])
                        if not P.plan_mode:
                            for hp in range(2):
                                P.emit('dve', lambda e, hp=hp: e.tensor_reduce(
                                    out=ST[0:4, 16 + hp * 2:18 + hp * 2], in_=ps3(pS[hp], 256)[0:4], axis=AX.X,
                                    op=ALU.max), [bPS[pS[hp]]], [bST])
                            P.emit('dve', lambda e: e.tensor_scalar(out=ST[0:4, 20:24], in0=ST[0:4, 16:20],
                                                                    scalar1=-SCALE, scalar2=None, op0=ALU.mult),
                                   [], [bST])
                            P.emit('dve', lambda e: e.memset(ST[0:4, 24:28], 0.0), [], [bST])
                            for h in range(4):
                                P.emit('act', lambda e, h=h: e.activation(
                                    out=Ef[0:4, h * 256:(h + 1) * 256],
                                    in_=PS[pS[h // 2]][0:4, (h % 2) * 256:(h % 2) * 256 + 256],
                                    func=AF.Exp, scale=SCALE, bias=ST[0:4, 20 + h:21 + h],
                                    accum_out=ST[0:4, 24 + h:25 + h]), [bPS[pS[h // 2]], bST], [bE, bST])
                            held.discard(pS[0])
                            held.discard(pS[1])
                            P.emit('dve', lambda e: e.reciprocal(out=ST[0:4, 28:32], in_=ST[0:4, 24:28]), [], [bST])
                            for h in range(4):
                                P.emit('dve', lambda e, h=h: e.tensor_scalar(
                                    out=Abf[0:4, h * 256:(h + 1) * 256], in0=Ef[0:4, h * 256:(h + 1) * 256],
                                    scalar1=ST[0:4, 28 + h:29 + h], scalar2=None, op0=ALU.mult), [bE, bST], [bA])
                            pi = psum()
                            for hm in range(8):
                                P.emit('pe', lambda e, pi=pi, hm=hm: e.matmul(
                                    PS[pi][:, hm * 4:(hm + 1) * 4], lhsT=Abf[0:4, hm * 128:(hm + 1) * 128],
                                    rhs=ident[0:4, 0:4], start=True, stop=True), [bA, bCONST], [bPS[pi]], sig=(hm == 7))
                            altcopy(ATb, PS[pi][:, 0:32], [bPS[pi]], [bA])
                        pO = psum()
                        for h in range(4):
                            d = dict(tag=('cv', b, h), srcs=[(cv[b, :, h * 1024:(h + 1) * 1024].rearrange(
                                "(m p) d -> p m d", p=128), (lambda stg: stg[:, :].rearrange("p (m d) -> p m d", d=1024)))],
                                cast=('plain',), np=128, ne=2048)
                            si = slab_get(d)
                            if P.plan_mode:
                                continue
                            for dc in range(8):
                                for mc in range(2):
                                    P.emit('pe', lambda e, h=h, dc=dc, mc=mc, si=si, pO=pO: e.matmul(
                                        PS[pO][:, (h * 8 + dc) * 4:(h * 8 + dc + 1) * 4],
                                        lhsT=SLAB[si][:, mc * 1024 + dc * 128:mc * 1024 + (dc + 1) * 128],
                                        rhs=ATb[:, (h * 2 + mc) * 4:(h * 2 + mc + 1) * 4], start=(mc == 0), stop=(mc == 1)),
                                        bSLAB[si] + [bA], [bPS[pO]], sig=(mc == 1 and dc == 7))
                        if not P.plan_mode:
                            altcopy(HT[:, :, q0:q0 + 4], ps3(pO, 4)[:, 0:32, :], [bPS[pO]], [bHT])
                    for b_ in range(16):
                        attn_sample_seq(b_)
                alias([bTMP], [bE, bA])

                stage(f'g{gi}p6')
                linear_B(('wo', gi), w_o, HT, bHT, 32, cb8, tiles, plain, res_cons)

                for i in range(3):
                    norm_T(bX[i], X[i], tiles[i][1], HT, bHT, tiles[i][0])

                stage(f'g{gi}p8')
                SG = TMP[:, 0:512]
                bSG = Buf("SG")
                alias([bSG], [bTMP])
                nblk = [8, 8, 8, 8, 8, 3]
                j0 = 0
                for fb, nj in enumerate(nblk):
                    pieces = [[((j0 + j) * 256, 256, 0), (DFF + (j0 + j) * 256, 256, 256)] for j in range(nj)]

                    def gu_cons(js, pis):
                        for c in range(2):
                            P.emit('act', lambda e, c=c: e.activation(out=SG[:, 0:Tg], in_=PS[pis[c]][:, 0:Tg], func=AF.Silu),
                                   [bPS[pis[c]]], [bSG])
                            P.emit('dve', lambda e, c=c: e.tensor_tensor(
                                out=B2[:, js * 2 + c, 0:Tg], in0=SG[:, 0:Tg], in1=PS[pis[c + 2]][:, 0:Tg], op=ALU.mult),
                                [bSG, bPS[pis[c + 2]]], [bB2])
                    linear_A(('wgu', gi, fb), w_gu, HT, bHT, 32, pieces, Tg, gcast(2), gu_cons)
                    linear_B(('wdn', gi, fb), w_down[j0 * 256:(j0 + nj) * 256, :], B2, bB2, nj * 2, cb8, tiles, plain,
                             res_cons)
                    j0 += nj
                alias([bTMP], [bSG])

                stage(f'g{gi}p9')
                P.dma('sp', 'misc', B2[:, :, :].rearrange("p a b -> p (a b)")[:, 0:8192].bitcast(F32), gfin_d, [], [bB2])
                GF = B2[:, :, :].rearrange("p a b -> p (a b)")[:, 0:8192].bitcast(F32)
                for i, (kind, idx) in enumerate(grp):
                    nt = tiles[i][1]
                    ss, sq, rs = ST[0:nt, 0:1], ST[0:nt, 1:2], ST[0:nt, 2:3]
                    P.emit('dve', lambda e: e.memset(ST[:, 0:1], 0.0), [], [bST])
                    P.emit('act', lambda e, i=i, nt=nt, ss=ss: e.activation(out=XN[0:nt, :], in_=X[i][0:nt, :], func=AF.Square,
                                                                     accum_out=ss), [bX[i]], [bR1, bST])
                    P.emit('act', lambda e, nt=nt, ss=ss, sq=sq: e.activation(out=sq, in_=ss, func=AF.Sqrt, scale=1.0 / D,
                                                                       bias=EPS_AP[0:nt]), [bCONST], [bST])
                    P.emit('dve', lambda e, sq=sq, rs=rs: e.reciprocal(out=rs, in_=sq), [], [bST])
                    P.emit('dve', lambda e, i=i, nt=nt, rs=rs: e.scalar_tensor_tensor(
                        out=X[i][0:nt, :], in0=X[i][0:nt, :], scalar=rs, in1=GF[0:nt, :], op0=ALU.mult, op1=ALU.mult),
                        [bST, bB2], [bX[i]])
                    dst = yp[idx * 128:(idx + 1) * 128, :] if kind == 'p' else ys
                    store(dst, X[i][0:nt, :], [bX[i]])

            for gi_, grp_ in enumerate(GROUPS):
                group_pass(gi_, grp_)


        EPS_AP = ST[:, 8:9]
        P0 = Prog(True)
        run_all(P0)
        P1 = Prog(False, P0.plan)
        run_all(P1)
        assert P1.si == len(P0.plan) or KSTOP, (P1.si, len(P0.plan))
        print("plan", len(P0.plan), {k: len(v.ops) for k, v in P1.E.items()}, {k: v.count for k, v in P1.E.items()}, flush=True)

        block = es.enter_context(nc.Block())

        def replay(engname):
            def f(e):
                for (waits, fn, upd) in P1.E[engname].ops:
                    for (k, v) in waits:
                        e.wait_ge(SEM[k], v)
                    if fn is None:
                        continue
                    ins = fn(e)
                    if upd is not None:
                        ins.then_inc(SEM[upd[0]], upd[1])
            return f

        block.tensor(replay('pe'))
        block.scalar(replay('act'))
        block.vector(replay('dve'))
        block.sync(replay('sp'))
    return nc


def _consts():
    bf = ml_dtypes.bfloat16
    cb = np.zeros((128, 2432), np.float32)
    cb[:, 0:128] = np.eye(128)

    def Bmat(a0, rows_off):
        out = np.zeros((4, 128, 128), np.float32)
        for g, w in enumerate(WINS):
            for tl in range(128):
                t = a0 + tl
                cnt = min(t + 1, w)
                for sl in range(128):
                    s = a0 + rows_off + sl
                    v = 0.0
                    if s >= 0 and t - w < s <= t:
                        v += 1.0 / cnt
                    if s == t:
                        v -= 1.0
                    out[g, sl, tl] = v
        return out
    bt0 = Bmat(0, 0)
    btm = Bmat(1024, 0)
    btp = Bmat(1024, -128)
    bsT = np.zeros((4, 64, 64), np.float32)
    bst = np.zeros((2, 4, 120, 64), np.float32)
    for g, w in enumerate(WINS):
        for b in range(16):
            for lt in range(4):
                t = b * 4 + lt
                for i in range(15 + lt - w + 1, 15 + lt + 1):
                    if i >= 15:
                        bsT[g, b * 4 + (i - 15), t] += 1.0 / w
                    else:
                        bst[b // 8, g, (b % 8) * 15 + i, t] += 1.0 / w
                bsT[g, t, t] -= 1.0
    return cb, bt0, btm, btp, bsT, bst


_CACHE = {}


def kernel(x_prompt, x_sample, cache_mem_k, cache_mem_v, state_pool, mem_prompt, g_mix, w_in, g_v, w_pool,
           pool_scale, w_s, b_s, w_out, g_xattn, g_mem, w_q, w_k, w_v, w_o, g_ffn, w_gate_up, w_down, g_final):
    f = lambda a: np.ascontiguousarray(np.asarray(a, dtype=np.float32))
    x_prompt, x_sample = f(x_prompt), f(x_sample)
    ckf = f(cache_mem_k)[0].reshape(128, NMEM, D)
    cvf = f(cache_mem_v)[0].reshape(128, NMEM, D)
    stf = f(state_pool)[0]
    memf = f(mem_prompt)
    if 'nc' not in _CACHE:
        _CACHE['nc'] = build_program()
    nc = _CACHE['nc']
    cb, bt0, btm, btp, bsT, bst = _consts()
    gcols = np.concatenate([f(g)[0].reshape(32, 128).T for g in (g_mix, g_xattn, g_ffn, g_mem)], axis=1)
    smallA = np.zeros((128, 152), np.float32)
    smallA[:, 0:128] = gcols
    smallA[:, 128:144] = f(pool_scale)[0].reshape(16, 128).T
    smallA[:, 144:148] = f(b_s)[0].T
    smallA[0:64, 148:152] = np.tile(f(b_s)[0][:, :4].T, (16, 1))
    ws = f(w_s)[0]
    smallB = np.zeros((128, 1536), np.float32)
    smallB[:, 0:512] = ws.transpose(2, 0, 1).reshape(128, 512)
    tril_T = np.triu(np.ones((128, 128), np.float32))
    smallB[:, 512:1024] = np.tile(tril_T, (1, 4))
    ws4 = ws[:, :4, :4].transpose(2, 0, 1)
    smallB[0:64, 1024:1280] = np.tile(ws4[:, :, None, :], (16, 1, 16, 1)).reshape(64, 256)
    m4 = np.triu(np.ones((4, 4), np.float32))
    smallB[0:64, 1280:1536] = np.tile(np.kron(np.eye(16, dtype=np.float32), m4)[:, None, :], (1, 4, 1)).reshape(64, 256)
    gfin_bc = np.ascontiguousarray(np.broadcast_to(f(g_final)[None, :], (128, D)))
    gv_bc = np.ascontiguousarray(np.broadcast_to(f(g_v)[0][None, :], (128, DG)))
    wd = dict(w_in=f(w_in)[0], w_pool=f(w_pool)[0].reshape(2048, 512), w_out=f(w_out)[0], w_q=f(w_q)[0],
              w_k=f(w_k)[0], w_v=f(w_v)[0], w_o=f(w_o)[0], w_gu=f(w_gate_up)[0], w_down=f(w_down)[0])
    def core_map(c):
        b, half = c // 2, c % 2
        cbc = cb.copy()
        first = bt0 if half == 0 else btm
        cbc[:, 128:640] = first.transpose(1, 0, 2).reshape(128, 512)
        cbc[:, 640:1152] = btm.transpose(1, 0, 2).reshape(128, 512)
        cbc[:, 1152:1664] = btp.transpose(1, 0, 2).reshape(128, 512)
        cbc[0:64, 1664:1920] = bsT.transpose(1, 0, 2).reshape(64, 256)
        cbc[0:120, 1920:2432] = bst.transpose(2, 0, 1, 3).reshape(120, 512)
        xh = x_prompt[b, 896:1024] if half == 1 else np.zeros((128, D), np.float32)
        m = dict(xp=x_prompt[b, half * 1024:(half + 1) * 1024], xh=np.ascontiguousarray(xh),
                 xs=x_sample[c * 16:(c + 1) * 16].reshape(64, D), memx=memf[b],
                 ck=ckf[c * 16:(c + 1) * 16], cv=cvf[c * 16:(c + 1) * 16], stp=stf[c * 16:(c + 1) * 16],
                 smallA=smallA, smallB=smallB, gfin_bc=gfin_bc, gv_bc=gv_bc, cbf=cbc.astype(ml_dtypes.bfloat16))
        m.update(wd)
        return {k: np.ascontiguousarray(v) for k, v in m.items()}
    if _CACHE.get('sim_core') is not None:
        return nc, core_map(_CACHE['sim_core'])
    if _CACHE.get('sim_cores') is not None:
        return nc, [core_map(c) for c in _CACHE['sim_cores']]
    in_maps = [core_map(c) for c in range(NCORES)]
    res = run_bass_kernel_spmd(nc, in_maps, core_ids=list(range(NCORES))).results
    y_prompt = np.zeros((4, 2048, D), np.float32)
    y_sample = np.zeros((128, 4, D), np.float32)
    mk = np.zeros((1, 4, NMEM, 4, 1024), np.float32)
    mv = np.zeros((1, 4, NMEM, 4, 1024), np.float32)
    psp = np.zeros((1, 4, 15, DP), np.float32)
    pss = np.zeros((1, 128, 15, DP), np.float32)
    gvs = np.zeros((1, 128, 4, DG), np.float32)
    for c in range(NCORES):
        b, half = c // 2, c % 2
        r = res[c]
        y_prompt[b, half * 1024:(half + 1) * 1024] = r["yp"]
        y_sample[c * 16:(c + 1) * 16] = r["ys"].reshape(16, 4, D)
        if half == 0:
            mk[0, b] = r["mk"].reshape(NMEM, 4, 1024)
            mv[0, b] = r["mv"].reshape(NMEM, 4, 1024)
        else:
            psp[0, b] = r["psp"]
        pss[0, c * 16:(c + 1) * 16] = r["pss"]
        gvs[0, c * 16:(c + 1) * 16] = r["gvs"].reshape(16, 4, DG)
    return (y_prompt, y_sample, mk, mv, psp, pss, gvs)
```
